# Optimizing a Trainium2 kernel written in Bass

```python
import math
import jax, jax.numpy as jnp
from jax import lax
import numpy as np

D_MODEL = 1024
BATCH = 8
SEQ = 4096
DEPTH = 1

GDN_HEADS = 4
GDN_HEAD_DIM = 128
GDN_WIDTH = GDN_HEADS * GDN_HEAD_DIM
CONV_WIDTH = 4
CHUNK = 64
SWA_HEADS = 8
SWA_HEAD_DIM = 64
SWA_WIDTH = SWA_HEADS * SWA_HEAD_DIM
DILATED_PATTERNS = ((128, 1), (512, 4), (2048, 16))
NUM_BUCKETS = 32
MAX_DISTANCE = 2048
MIX_WIDTH = GDN_WIDTH + SWA_WIDTH
IN_SIZES = (GDN_WIDTH, GDN_WIDTH, GDN_WIDTH, GDN_WIDTH, GDN_HEADS, GDN_HEADS,
            SWA_WIDTH, SWA_WIDTH, SWA_WIDTH)
IN_COLS = sum(IN_SIZES)
D_FF = ((-(-8 * D_MODEL // 3)) + 255) // 256 * 256
RMS_EPS = 1e-6

kernel_name = 'hybrid_gdn_dilated_swa_block'


def rmsnorm(x, g):
    xf = x.astype(jnp.float32)
    y = xf * lax.rsqrt(jnp.mean(xf * xf, axis=-1, keepdims=True) + RMS_EPS)
    return (y * g.astype(jnp.float32)).astype(x.dtype)


def l2norm(x):
    return x * lax.rsqrt(jnp.sum(x * x, axis=-1, keepdims=True) + 1e-6)


def causal_depthwise_conv(x, w):
    K, C = w.shape
    return lax.conv_general_dilated(x, w[:, None, :].astype(x.dtype), (1,), ((K - 1, 0),),
                                    dimension_numbers=('NWC', 'WIO', 'NWC'),
                                    feature_group_count=C)


def chunk_gated_delta_rule(q, k, v, g, beta):
    Bn, S, H, Dk = q.shape
    Dv = v.shape[-1]
    N = S // CHUNK

    def chunks(t):
        t = t.reshape((Bn, N, CHUNK, H) + t.shape[3:])
        return jnp.moveaxis(jnp.moveaxis(t, 1, 0), 3, 2)

    qc, kc, vc = chunks(q), chunks(k), chunks(v)
    bc = chunks(beta)
    gc = jnp.cumsum(chunks(g), axis=-1)
    idx = jnp.arange(CHUNK)
    causal = idx[:, None] >= idx[None, :]
    strict = idx[:, None] > idx[None, :]
    decay = jnp.exp(jnp.where(causal, gc[..., :, None] - gc[..., None, :], -jnp.inf))
    kb = kc * bc[..., None]
    a = jnp.where(strict, jnp.einsum('nbhid,nbhjd->nbhij', kb, kc) * decay, 0.0)
    eye = jnp.eye(CHUNK, dtype=q.dtype)
    t_inv = lax.linalg.triangular_solve(eye + a, jnp.broadcast_to(eye, a.shape),
                                        left_side=True, lower=True)
    u = t_inv @ (vc * bc[..., None])
    w = t_inv @ (kb * jnp.exp(gc)[..., None])
    attn = jnp.einsum('nbhid,nbhjd->nbhij', qc, kc) * decay
    q_dec = qc * jnp.exp(gc)[..., None]
    k_dec = kc * jnp.exp(gc[..., -1:] - gc)[..., None]
    g_end = jnp.exp(gc[..., -1])

    def step(state, inp):
        u_i, w_i, attn_i, qd_i, kd_i, ge_i = inp
        v_new = u_i - jnp.einsum('bhck,bhkv->bhcv', w_i, state)
        o_i = (jnp.einsum('bhck,bhkv->bhcv', qd_i, state)
               + jnp.einsum('bhij,bhjv->bhiv', attn_i, v_new))
        state = state * ge_i[..., None, None] + jnp.einsum('bhck,bhcv->bhkv', kd_i, v_new)
        return state, o_i

    s0 = jnp.zeros((Bn, H, Dk, Dv), q.dtype)
    _, o = lax.scan(step, s0, (u, w, attn, q_dec, k_dec, g_end))
    return jnp.moveaxis(o, 0, 1).transpose(0, 1, 3, 2, 4).reshape(Bn, S, H, Dv)


def gated_deltanet(q, k, v, gate, beta_logit, a_logit, conv_w, a_log, dt_bias, onorm_g):
    f32 = jnp.float32
    Bn, S, _ = q.shape
    qkv = jax.nn.silu(causal_depthwise_conv(jnp.concatenate([q, k, v], axis=-1), conv_w)).astype(f32)
    q, k, v = jnp.split(qkv, 3, axis=-1)
    heads = lambda t: t.reshape(Bn, S, GDN_HEADS, GDN_HEAD_DIM)
    q = l2norm(heads(q)) * GDN_HEAD_DIM ** -0.5
    k = l2norm(heads(k))
    v = heads(v)
    beta = jax.nn.sigmoid(beta_logit.astype(f32))
    g = -jnp.exp(a_log.astype(f32)) * jax.nn.softplus(a_logit.astype(f32) + dt_bias.astype(f32))
    o = chunk_gated_delta_rule(q, k, v, g, beta)
    o = o * lax.rsqrt(jnp.mean(o * o, axis=-1, keepdims=True) + RMS_EPS) * onorm_g.astype(f32)
    o = o * jax.nn.silu(heads(gate).astype(f32))
    return o.reshape(Bn, S, GDN_WIDTH).astype(gate.dtype)


def t5_causal_bucket(dist):
    max_exact = NUM_BUCKETS // 2
    d = jnp.maximum(dist, 1).astype(jnp.float32)
    log_b = max_exact + (jnp.log(d / max_exact) / math.log(MAX_DISTANCE / max_exact)
                         * (NUM_BUCKETS - max_exact)).astype(jnp.int32)
    return jnp.where(dist < max_exact, dist, jnp.minimum(log_b, NUM_BUCKETS - 1))


def dilated_band(q, k, v, rel_bias, window, dilation):
    Bn, S, H, Dh = q.shape
    w = window // dilation
    L = S // dilation
    nb = -(-L // w)
    Lp = nb * w

    def to_sub(t):
        t = t.reshape(Bn, L, dilation, H, Dh).transpose(0, 3, 2, 1, 4)
        return jnp.pad(t, ((0, 0), (0, 0), (0, 0), (0, Lp - L), (0, 0)))

    blocks = lambda t: t.reshape(Bn, H, dilation, nb, w, Dh)

    def band(t):
        prev = jnp.pad(t, ((0, 0), (0, 0), (0, 0), (w, 0), (0, 0)))[:, :, :, :Lp]
        return jnp.concatenate([blocks(prev), blocks(t)], axis=-2)

    qb, kb, vb = blocks(to_sub(q)), band(to_sub(k)), band(to_sub(v))
    qi = jnp.arange(w)[:, None]
    kj = jnp.arange(2 * w)[None, :]
    rel = qi + w - kj
    blk = jnp.arange(nb)[:, None, None]
    valid = (rel >= 0) & (rel <= w) & (blk * w + kj - w >= 0)
    bias_steps = rel_bias[t5_causal_bucket(jnp.arange(w + 1) * dilation)]
    bias = jnp.moveaxis(bias_steps[jnp.clip(rel, 0, w)], -1, 0).astype(jnp.float32)
    logits = jnp.einsum('bhrnqd,bhrnkd->bhrnqk', qb, kb) * Dh ** -0.5 + bias[None, :, None, None]
    logits = jnp.where(valid, logits, -jnp.inf)
    m = jnp.max(logits, axis=-1)
    p = jnp.exp(logits - m[..., None])
    s = jnp.sum(p, axis=-1)
    num = jnp.einsum('bhrnqk,bhrnkd->bhrnqd', p, vb)

    def from_sub(t):
        rest = t.shape[5:]
        t = t.reshape((Bn, H, dilation, Lp) + rest)[:, :, :, :L]
        t = t.transpose((0, 3, 2, 1) + tuple(range(4, t.ndim)))
        return t.reshape((Bn, S, H) + rest)

    return from_sub(num), from_sub(m), from_sub(s)


def dilated_attention(q, k, v, rel_bias):
    Bn, S, _ = q.shape
    heads = lambda t: t.reshape(Bn, S, SWA_HEADS, SWA_HEAD_DIM).astype(jnp.float32)
    qh, kh, vh = heads(q), heads(k), heads(v)
    parts = [dilated_band(qh, kh, vh, rel_bias, win, dil) for win, dil in DILATED_PATTERNS]
    nums = jnp.stack([pt[0] for pt in parts])
    ms = jnp.stack([pt[1] for pt in parts])
    ss = jnp.stack([pt[2] for pt in parts])
    wts = jnp.exp(ms - jnp.max(ms, axis=0, keepdims=True))
    out = jnp.sum(wts[..., None] * nums, axis=0) / jnp.sum(wts * ss, axis=0)[..., None]
    return out.reshape(Bn, S, SWA_WIDTH).astype(q.dtype)


def setup_inputs(seed: int = 0) -> dict:
    key = jax.random.key(seed)
    ks = jax.random.split(key, 16)
    f32 = jnp.float32
    nrm = lambda k_, shape, scale: jax.random.normal(k_, shape, f32) * scale
    dt = jnp.exp(jax.random.uniform(ks[4], (DEPTH, GDN_HEADS), f32, math.log(1e-3), math.log(1e-1)))
    return {
        'x': nrm(ks[0], (BATCH, SEQ, D_MODEL), 1.0),
        'w_in': nrm(ks[1], (DEPTH, D_MODEL, IN_COLS), D_MODEL ** -0.5),
        'conv_w': nrm(ks[2], (DEPTH, CONV_WIDTH, 3 * GDN_WIDTH), CONV_WIDTH ** -0.5),
        'a_log': jnp.log(jax.random.uniform(ks[3], (DEPTH, GDN_HEADS), f32, 1.0, 16.0)),
        'dt_bias': dt + jnp.log(-jnp.expm1(-dt)),
        'onorm_g': 1.0 + nrm(ks[5], (DEPTH, GDN_HEAD_DIM), 0.1),
        'rel_bias': nrm(ks[6], (NUM_BUCKETS, SWA_HEADS), 0.2),
        'w_out': nrm(ks[7], (DEPTH, MIX_WIDTH, D_MODEL), MIX_WIDTH ** -0.5),
        'g_mix_pre': 1.0 + nrm(ks[8], (DEPTH, D_MODEL), 0.1),
        'g_mix_post': 1.0 + nrm(ks[9], (DEPTH, D_MODEL), 0.1),
        'w_gate': nrm(ks[10], (DEPTH, D_MODEL, D_FF), D_MODEL ** -0.5),
        'w_up': nrm(ks[11], (DEPTH, D_MODEL, D_FF), D_MODEL ** -0.5),
        'w_down': nrm(ks[12], (DEPTH, D_FF, D_MODEL), D_FF ** -0.5),
        'g_ffn_pre': 1.0 + nrm(ks[13], (DEPTH, D_MODEL), 0.1),
        'g_ffn_post': 1.0 + nrm(ks[14], (DEPTH, D_MODEL), 0.1),
    }


def reference(x, w_in, conv_w, a_log, dt_bias, onorm_g, rel_bias, w_out, g_mix_pre, g_mix_post,
              w_gate, w_up, w_down, g_ffn_pre, g_ffn_post):
    split_at = [int(c) for c in np.cumsum(IN_SIZES)[:-1]]
    for l in range(DEPTH):
        h = rmsnorm(x, g_mix_pre[l])
        proj = h @ w_in[l]
        qa, ka, va, gate_a, beta_a, alpha_a, qb, kb, vb = jnp.split(proj, split_at, axis=-1)
        out_a = gated_deltanet(qa, ka, va, gate_a, beta_a, alpha_a,
                               conv_w[l], a_log[l], dt_bias[l], onorm_g[l])
        out_b = dilated_attention(qb, kb, vb, rel_bias)
        mix = jnp.concatenate([out_a, out_b], axis=-1) @ w_out[l]
        x = x + rmsnorm(mix, g_mix_post[l])
        h = rmsnorm(x, g_ffn_pre[l])
        f = (jax.nn.silu(h @ w_gate[l]) * (h @ w_up[l])) @ w_down[l]
        x = x + rmsnorm(f, g_ffn_post[l])
    return x
```

```python
import math
import numpy as np
from contextlib import ExitStack
import concourse.bass as bass
import concourse.mybir as mybir
from concourse.bass_utils import run_bass_kernel_spmd

F32 = mybir.dt.float32
BF16 = mybir.dt.bfloat16
AF = mybir.ActivationFunctionType
ALU = mybir.AluOpType

NDS = 24
NDP = 8
SEQ = 4096
DM = 1024
DFF = 2816
NEG = -30000.0
EPS = 1e-6


_PSUM_KEYS = ('G', 'pj', 'sp', 'nz', 'vtp', 'tp', 'B0', 'F')


def _is_psum_key(k):
    return k == 'B0' or (isinstance(k, tuple) and k[0] in _PSUM_KEYS)


class Sched:
    def __init__(self, nc, es):
        self.nc = nc
        self.engs = {'pe': nc.tensor, 'act': nc.scalar, 'dve': nc.vector,
                     'pool': nc.gpsimd, 'sp': nc.sync}
        self.sem = {k: es.enter_context(nc.semaphore('s_' + k)) for k in self.engs}
        self.cnt = {k: 0 for k in self.engs}
        self.waited = {k: {} for k in self.engs}
        self.dsem = [es.enter_context(nc.semaphore('d%d' % i)) for i in range(NDS + NDP)]
        self.dval = [0] * (NDS + NDP)
        self.dnext = 0
        self.pnext = 0
        self.last_w = {}
        self.readers = {}
        self.prog = {k: [] for k in self.engs}
        self.nins = {k: 0 for k in self.engs}

    def _semof(self, sk):
        return self.sem[sk] if isinstance(sk, str) else self.dsem[sk[1]]

    def wait(self, eng, tok):
        sk, val = tok
        if sk == eng and eng == 'pe':
            return
        if self.waited[eng].get(sk, 0) >= val:
            return
        sem = self._semof(sk)
        self.prog[eng].append(lambda e, sem=sem, val=val: e.wait_ge(sem, val))
        self.waited[eng][sk] = val

    def _deps(self, eng, reads, writes):
        deps = {}

        def add(tok, war=False):
            sk, val = tok
            if war and sk == eng and eng == 'pe':
                return
            if deps.get(sk, 0) < val:
                deps[sk] = val
        for k in reads:
            if k in self.last_w:
                add(self.last_w[k])
            if _is_psum_key(k):
                for sk, val in self.readers.get(k, {}).items():
                    if sk != eng:
                        add((sk, val))
        for k in writes:
            if k in self.last_w:
                add(self.last_w[k], war=True)
            for sk, val in self.readers.get(k, {}).items():
                add((sk, val), war=True)
        for sk, val in deps.items():
            self.wait(eng, (sk, val))

    def _record(self, tok, reads, writes):
        sk, val = tok
        for k in reads:
            self.readers.setdefault(k, {})[sk] = val
        for k in writes:
            self.last_w[k] = tok
            self.readers[k] = {}

    def op(self, eng, meth, *args, reads=(), writes=(), **kw):
        self._deps(eng, reads, writes)
        sem = self.sem[eng]
        self.prog[eng].append(lambda e, meth=meth, args=args, kw=kw, sem=sem:
                              getattr(e, meth)(*args, **kw).then_inc(sem, 1))
        self.cnt[eng] += 1
        self.nins[eng] += 1
        self._record((eng, self.cnt[eng]), reads, writes)

    def mm(self, items, reads=(), writes=()):
        self._deps('pe', reads, writes)
        sem = self.sem['pe']
        items = list(items)

        def run(e, items=items, sem=sem):
            ins = None
            for it in items:
                ins = e.matmul(it[0], it[1], it[2], start=it[3], stop=it[4])
            ins.then_inc(sem, 1)
        self.prog['pe'].append(run)
        self.cnt['pe'] += 1
        self.nins['pe'] += len(items)
        self._record(('pe', self.cnt['pe']), reads, writes)

    def tr(self, items, reads=(), writes=()):
        self._deps('pe', reads, writes)
        sem = self.sem['pe']
        items = list(items)

        def run(e, items=items, sem=sem):
            ins = None
            for it in items:
                ins = e.transpose(it[0], it[1], it[2])
            ins.then_inc(sem, 1)
        self.prog['pe'].append(run)
        self.cnt['pe'] += 1
        self.nins['pe'] += len(items)
        self._record(('pe', self.cnt['pe']), reads, writes)

    def dma(self, out, in_, reads=(), writes=(), eng='sp', **kw):
        self._deps(eng, reads, writes)
        if eng == 'pool':
            i = NDS + self.pnext
            self.pnext = (self.pnext + 1) % NDP
        else:
            i = self.dnext
            self.dnext = (self.dnext + 1) % NDS
        if self.dval[i] > 0:
            self.wait(eng, (('d', i), self.dval[i]))
        dsem = self.dsem[i]
        self.prog[eng].append(lambda e, out=out, in_=in_, kw=kw, dsem=dsem:
                              e.dma_start(out=out, in_=in_, **kw).then_inc(dsem, 16))
        self.dval[i] += 16
        self.nins[eng] += 1
        self._record((('d', i), self.dval[i]), reads, writes)

    def barrier(self):
        for e in self.engs:
            for o in self.engs:
                if o != e and self.cnt[o] > 0:
                    self.wait(e, (o, self.cnt[o]))
            for i in range(NDS + NDP):
                if self.dval[i] > 0:
                    self.wait(e, (('d', i), self.dval[i]))
        self.last_w = {}
        self.readers = {}

    def emit(self):
        with self.nc.Block() as block:
            for name, reg in (('pe', block.tensor), ('act', block.scalar), ('dve', block.vector),
                              ('pool', block.gpsimd), ('sp', block.sync)):
                prog = self.prog[name]

                def body(e, prog=prog):
                    for f in prog:
                        f(e)
                reg(body)


def build_nc(stop=9, dbg=False, ngroups=8):
    nc = bass.Bass("TRN2", target_bir_lowering=False)

    def D(name, shape, dt=F32, kind="ExternalInput"):
        return nc.dram_tensor(name, shape, dt, kind=kind).ap()
    x = D("x", [SEQ, DM])
    w_in = D("w_in", [DM, 3592])
    w_out = D("w_out", [DM, DM])
    w_gate = D("w_gate", [DM, DFF])
    w_up = D("w_up", [DM, DFF])
    w_down = D("w_down", [DFF, DM])
    gpre = D("gpre", [128, 16])
    gpost = D("gpost", [2, DM])
    convw = D("convw", [128, 48])
    hpar = D("hpar", [128, 16])
    biasT = D("biasT", [4, 128, 1536])
    consts = D("consts", [128, 1280])
    out = D("out", [SEQ, DM], kind="ExternalOutput")
    scr = D("scr_ob", [128, 4, SEQ], BF16, kind="Internal")
    dbg_t = {}
    if dbg:
        dbg_t['ob'] = D("dbg_ob", [128, 4, SEQ], BF16, kind="ExternalOutput")
        dbg_t['oa'] = D("dbg_oa", [128, 4, SEQ], BF16, kind="ExternalOutput")

    es = ExitStack()
    with es:
        S = Sched(nc, es)

        def sbt(st, name, shape, dt):
            return st.enter_context(nc.sbuf_tensor(name, shape, dt))

        def pst(st, name, shape, dt):
            return st.enter_context(nc.psum_tensor(name, shape, dt))

        cst = sbt(es, "cst", [128, 1280], F32)
        identf = cst[:, 0:128]
        Umat = cst[:, 128:256]
        maskneg = cst[:, 256:384]
        identb = sbt(es, "identb", [128, 128], BF16)
        onesb = sbt(es, "onesb", [128, 128], BF16)
        onesf = sbt(es, "onesf", [128, 128], F32)
        gpre_t = sbt(es, "gpre_t", [128, 16], F32)
        hpar_t = sbt(es, "hpar_t", [128, 16], F32)
        st = sbt(es, "stats", [128, 64], F32)
        junk = sbt(es, "junk", [128, 1024], BF16)
        S.dma(cst[:], consts, writes=['cst'])
        S.dma(gpre_t[:], gpre, writes=['gpre'])
        S.dma(hpar_t[:], hpar, writes=['hpar'])
        S.op('dve', 'tensor_copy', identb[:], identf, reads=['cst'], writes=['identb'])
        S.op('pool', 'memset', onesb[:], 1.0, writes=['onesb'])
        S.op('pool', 'memset', onesf[:], 1.0, writes=['onesf'])

        nst = [0, 0]

        def norm_T(src, dst, xt_t, xn_t, tp_t, slot, rkeys, wkeys, xkey=None, tpkey=None, g0=0, dve_scale=False):
            i = nst[0] % 8
            nst[0] += 1
            c = 4 * i
            kx = ('xt', slot) if xkey is None else xkey
            ktp = ('tp', slot) if tpkey is None else tpkey
            S.dma(xt_t[:], src, reads=rkeys, writes=[kx])
            S.op('act', 'activation', junk[:], xt_t[:], AF.Square, accum_out=st[:, c:c + 1],
                 reads=[kx], writes=['junk', ('st', i)])
            S.op('act', 'activation', st[:, c + 1:c + 2], st[:, c:c + 1], AF.Ln, bias=EPS, scale=1.0 / DM,
                 reads=[('st', i)], writes=[('st', i)])
            S.op('act', 'activation', st[:, c + 2:c + 3], st[:, c + 1:c + 2], AF.Exp, scale=-0.5,
                 reads=[('st', i)], writes=[('st', i)])
            if dve_scale:
                S.op('dve', 'tensor_scalar_mul', xn_t[:], xt_t[:], st[:, c + 2:c + 3],
                     reads=[kx, ('st', i)], writes=[('xn', slot)])
            else:
                S.op('act', 'activation', xn_t[:], xt_t[:], AF.Copy, scale=st[:, c + 2:c + 3],
                     reads=[kx, ('st', i)], writes=[('xn', slot)])
            S.tr([(tp_t[:, k * 128:(k + 1) * 128], xn_t[:, k * 128:(k + 1) * 128], identb[:]) for k in range(8)],
                 reads=[('xn', slot), 'identb'], writes=[ktp])
            S.op('dve', 'tensor_tensor', dst, tp_t[:].rearrange("p (k t) -> p k t", k=8),
                 gpre_t[:, g0:g0 + 8].unsqueeze(2).to_broadcast([128, 8, 128]), ALU.mult,
                 reads=[ktp, 'gpre'], writes=wkeys)

        def load_w(dst, dkey, w_ap, c0, ncols, nk, stage=None, gcol0=None, r0=0):
            src = w_ap[r0:r0 + nk * 128, :].rearrange("(kc p) c -> p kc c", p=128)[:, :, c0:c0 + ncols]
            S.dma(dst, src, writes=[dkey + ('a',), dkey + ('b',)], eng='pool')

        def wk(dkey):
            return [dkey + ('a',), dkey + ('b',)]

        def post_norm_residual(m0, m1, mkeys, res_src, res_keys, gp_t, dst, dst_keys, xt_t, tmp_t, slot, tslot=None):
            i = nst[0] % 8
            nst[0] += 1
            c = 4 * i
            ts_ = slot if tslot is None else tslot
            S.dma(xt_t[:], res_src, reads=res_keys, writes=[('xt', slot)])
            S.op('act', 'activation', junk[:, 0:512], m0, AF.Square, accum_out=st[:, c:c + 1],
                 reads=[mkeys[0]], writes=['junk', ('st', i)])
            S.op('act', 'activation', junk[:, 512:1024], m1, AF.Square, accum_out=st[:, c + 1:c + 2],
                 reads=[mkeys[1]], writes=['junk', ('st', i)])
            S.op('dve', 'tensor_tensor', st[:, c + 2:c + 3], st[:, c:c + 1], st[:, c + 1:c + 2], ALU.add,
                 reads=[('st', i)], writes=[('st', i)])
            S.op('act', 'activation', st[:, c + 3:c + 4], st[:, c + 2:c + 3], AF.Ln, bias=EPS, scale=1.0 / DM,
                 reads=[('st', i)], writes=[('st', i)])
            S.op('act', 'activation', st[:, c:c + 1], st[:, c + 3:c + 4], AF.Exp, scale=-0.5,
                 reads=[('st', i)], writes=[('st', i)])
            S.op('dve', 'scalar_tensor_tensor', tmp_t[:, 0:512], m0, st[:, c:c + 1], gp_t[:, 0:512],
                 ALU.mult, ALU.mult, reads=[mkeys[0], ('st', i), 'gp'], writes=[('tmpa', ts_)])
            S.op('dve', 'scalar_tensor_tensor', tmp_t[:, 512:1024], m1, st[:, c:c + 1], gp_t[:, 512:1024],
                 ALU.mult, ALU.mult, reads=[mkeys[1], ('st', i), 'gp'], writes=[('tmpb', ts_)])
            S.op('pool', 'tensor_tensor', tmp_t[:], tmp_t[:], xt_t[:], ALU.add,
                 reads=[('tmpa', ts_), ('tmpb', ts_), ('xt', slot)], writes=[('tmpa', ts_), ('tmpb', ts_)])
            S.dma(dst, tmp_t[:], reads=[('tmpa', ts_), ('tmpb', ts_)], writes=dst_keys, eng='pool')

        with ExitStack() as sA:
            hT = sbt(sA, "hT", [128, 8, SEQ], BF16)
            with ExitStack() as s1:
                xt = [sbt(s1, "xt%d" % i, [128, DM], F32) for i in range(3)]
                xn = [sbt(s1, "xn%d" % i, [128, DM], BF16) for i in range(2)]
                tp = [pst(s1, "tp%d" % i, [128, DM], BF16) for i in range(2)]
                for t in range(32):
                    norm_T(x[t * 128:(t + 1) * 128, :], hT[:, :, t * 128:(t + 1) * 128],
                           xt[t % 3], xn[t % 2], tp[t % 2], t % 2, [], [('hT', t)], xkey=('xt3', t % 3), dve_scale=True)
                S.barrier()
            if stop >= 2:
                phase2(nc, S, sbt, pst, hT, w_in, biasT, scr, dbg_t, identb, load_w, wk)
        S.barrier()
        C = dict(x=x, w_in=w_in, w_out=w_out, w_gate=w_gate, w_up=w_up, w_down=w_down, convw=convw, gpost=gpost,
                 scr=scr, out=out, dbg_t=dbg_t, identb=identb, identf=identf, Umat=Umat, maskneg=maskneg,
                 onesb=onesb, onesf=onesf, hpar_t=hpar_t, cst=cst, load_w=load_w, wk=wk, norm_T=norm_T,
                 post_norm_residual=post_norm_residual, gpre_t=gpre_t, ngroups=ngroups)
        if stop >= 3:
            phase3(nc, S, sbt, pst, C)
        if stop >= 5:
            phase5(nc, S, sbt, pst, C)

        S.barrier()
        for i in range(NDS + NDP):
            if S.dval[i] > 0:
                S.wait('sp', (('d', i), S.dval[i]))
        S.emit()
        print("instr counts", S.nins)
    return nc


def phase3(nc, S, sbt, pst, C):
    x, w_in, w_out, convw, gpost, scr, out, dbg_t = C['x'], C['w_in'], C['w_out'], C['convw'], C['gpost'], C['scr'], C['out'], C['dbg_t']
    identb, identf, Umat, maskneg, onesb, onesf = C['identb'], C['identf'], C['Umat'], C['maskneg'], C['onesb'], C['onesf']
    hpar_t, cst, load_w, wk, norm_T, post_norm_residual = C['hpar_t'], C['cst'], C['load_w'], C['wk'], C['norm_T'], C['post_norm_residual']
    NG = C.get('ngroups', 8)
    with ExitStack() as s3:
        Wg = sbt(s3, "Wg", [128, 8, 2056], BF16)
        Wo = sbt(s3, "Wo", [128, 8, 1024], BF16)
        diag = sbt(s3, "diag", [128, 48, 128], BF16)
        cw = sbt(s3, "cw", [128, 48], F32)
        gpb = sbt(s3, "gpb", [128, 1024], F32)
        negA = sbt(s3, "negA", [128, 4], F32)
        masks = sbt(s3, "masks", [128, 7, 128], BF16)
        Sf = sbt(s3, "Sf", [128, 4, 128], F32)
        Sb = sbt(s3, "Sb", [128, 4, 128], BF16)
        hist = sbt(s3, "hist", [128, 12, 3], BF16)
        xt = [sbt(s3, "xt3_%d" % i, [128, DM], F32) for i in range(2)]
        xn = [sbt(s3, "xn3_%d" % i, [128, DM], BF16) for i in range(2)]
        tmp1 = sbt(s3, "tmp3_0", [128, DM], F32)
        tmp = [tmp1, tmp1]
        hTg = sbt(s3, "hTg", [128, 8, 512], BF16)
        xc = [sbt(s3, "xc%d" % i, [128, 515], BF16) for i in range(2)]
        csL = [sbt(s3, "cs%d" % i, [128, 512], F32) for i in range(4)]
        sqL = [sbt(s3, "sq%d" % i, [128, 512], BF16) for i in range(4)]
        sdL = [sbt(s3, "sd%d" % i, [128, 512], F32) for i in range(4)]
        qT = sbt(s3, "qT", [128, 4, 512], BF16)
        kT = sbt(s3, "kT", [128, 4, 512], BF16)
        vT = sbt(s3, "vT", [128, 4, 512], BF16)
        sgate = sbt(s3, "sgate", [128, 4, 512], BF16)
        obT = sbt(s3, "obT", [128, 4, 512], BF16)
        oaT = sbt(s3, "oaT", [128, 4, 512], BF16)
        sm = sbt(s3, "sm", [128, 8, 16], F32)
        bt16 = [{n: sbt(s3, "c%d_%s" % (sl_, n), [128, 4, 128], BF16) for n in
                 ('vb', 'kbg', 'ktok', 'kdec', 'attnT', 'AT', 'qdecT', 'Tm', 'TT', 'Xs', 'wT', 'vnew', 'Ao0', 'Ao1', 'Ao2')}
                for sl_ in range(2)]
        ft32 = [{n: sbt(s3, "c%d_%s" % (sl_, n), [128, 4, 128], F32) for n in
                 ('Gb', 'diagB', 'Dm', 'DTm', 'egc', 'u')} for sl_ in range(2)]
        G = [pst(s3, "G%d" % i, [128, 512], F32) for i in range(6)]
        Bb = [pst(s3, "B%d" % i, [128, 1024], BF16) for i in range(2)]
        B0 = Bb[0]

        def g4(k):
            return G[k][:].rearrange("p (h c) -> p h c", h=4)

        def bc3(ap2):
            return ap2.unsqueeze(2).to_broadcast([128, 4, 128])

        def bch(ap2):
            return ap2.unsqueeze(1).to_broadcast([128, 4, 128])

        S.dma(cw[:], convw, writes=['cw'])
        S.dma(gpb[:], gpost[0:1, :].partition_broadcast(128), writes=['gp'])
        for l in range(7):
            S.op('dve', 'tensor_copy', masks[:, l, :], cst[:, 384 + l * 128:384 + (l + 1) * 128], reads=['cst'], writes=['masks'])
        S.op('pool', 'memset', Sf[:], 0.0, writes=['Sf'])
        S.op('pool', 'memset', Sb[:], 0.0, writes=['Sb'])
        S.op('pool', 'memset', hist[:], 0.0, writes=[('hist', c) for c in range(12)])
        S.op('act', 'activation', negA[:], hpar_t[:, 0:4], AF.Exp, reads=['hpar'], writes=['negA'])
        S.op('dve', 'tensor_scalar_mul', negA[:], negA[:], -1.0, reads=['negA'], writes=['negA'])
        for ci in range(48):
            S.op('act', 'activation', diag[:, ci, :], identf, AF.Copy, scale=cw[:, ci:ci + 1],
                 reads=['cst', 'cw'], writes=[('diag', ci)])
        load_w(Wg[:, :, 2048:2056], ('Wg', 4), w_in, 2048, 8, 8)
        for pc_ in range(4):
            load_w(Wg[:, :, pc_ * 512:(pc_ + 1) * 512], ('Wg', pc_), w_in, pc_ * 512, 512, 8)
        for pc_ in range(2):
            load_w(Wo[:, :, pc_ * 512:(pc_ + 1) * 512], ('Wo', pc_), w_out, pc_ * 512, 512, 8)
        print("phase3 sbuf bytes remaining", nc.sbuf_bytes_remaining)
        WoK = [k for pc_ in range(2) for k in wk(('Wo', pc_))]

        zt, ez, gt, eb, btt, gct, egt, bgt = [sm[:, i, :].rearrange("p (a b) -> p a b", a=4) for i in range(8)]

        import os
        L3 = int(os.environ.get("P3LIM", "99"))
        for tg in range(NG):
            if L3 < 2:
                break
            tiles = [tg * 4 + i for i in range(4)]
            S.dma(obT[:], scr[:, :, tg * 512:(tg + 1) * 512], reads=[('scr', tg)], writes=['obT'])
            for i, t in enumerate(tiles):
                norm_T(x[t * 128:(t + 1) * 128, :], hTg[:, :, i * 128:(i + 1) * 128],
                       xt[i % 2], xn[i % 2], B0, i % 2, [], [('hTg', i)], tpkey='B0')
            hk = [('hTg', i) for i in range(4)]
            S.mm([(G[5][:, i * 8:(i + 1) * 8], hTg[:, kc, i * 128:(i + 1) * 128], Wg[:, kc, 2048:2056], kc == 0, kc == 7)
                  for i in range(4) for kc in range(8)], reads=hk + wk(('Wg', 4)), writes=[('G', 5)])
            ba = G[5][:, 0:32].rearrange("p (i c) -> p i c", i=4)
            S.op('dve', 'tensor_tensor', zt, ba[:, :, 4:8], hpar_t[:, 4:8].unsqueeze(1).to_broadcast([128, 4, 4]), ALU.add,
                 reads=[('G', 5), 'hpar'], writes=['zt'])
            S.op('act', 'activation', ez, zt, AF.Exp, reads=['zt'], writes=['ez'])
            S.op('act', 'activation', zt, ez, AF.Ln, bias=1.0, reads=['ez', 'zt'], writes=['zt'])
            S.op('dve', 'tensor_tensor', gt, zt, negA[:].unsqueeze(1).to_broadcast([128, 4, 4]), ALU.mult,
                 reads=['zt', 'negA'], writes=['gt'])
            S.op('act', 'activation', eb, ba[:, :, 0:4], AF.Exp, scale=-1.0, reads=[('G', 5)], writes=['eb'])
            S.op('dve', 'tensor_scalar_add', eb, eb, 1.0, reads=['eb'], writes=['eb'])
            S.op('dve', 'reciprocal', btt, eb, reads=['eb'], writes=['bt'])
            S.mm([(G[4][:, 0:16], Umat, sm[:, 2, :], True, True)], reads=['gt', 'cst'], writes=[('G', 4)])
            S.op('dve', 'tensor_copy', sm[:, 5, :], G[4][:, 0:16], reads=[('G', 4)], writes=['gct'])
            S.op('act', 'activation', egt, gct, AF.Exp, reads=['gct'], writes=['egt'])
            S.op('dve', 'tensor_tensor', bgt, btt, egt, ALU.mult, reads=['bt', 'egt'], writes=['bgt'])
            if L3 < 3:
                continue
            def stA(c):
                pj = G[c % 2]
                S.mm([(pj[:], Wg[:, kc, c * 128:(c + 1) * 128], hTg[:, kc, :], kc == 0, kc == 7) for kc in range(8)],
                     reads=hk + wk(('Wg', c // 4)), writes=[('G', c % 2)])
                if c >= 12:
                    S.op('act', 'activation', sgate[:, c - 12, :], pj[:], AF.Silu, reads=[('G', c % 2)], writes=[('sgate', c - 12)])
                    return
                xcb = xc[c % 2]
                kxc = ('xc', c % 2)
                S.op('pool', 'tensor_copy', xcb[:, 0:3], hist[:, c, :], reads=[('hist', c)], writes=[kxc])
                S.op('act', 'copy', xcb[:, 3:515], pj[:], reads=[('G', c % 2)], writes=[kxc + ('m',)])
                S.op('pool', 'tensor_copy', hist[:, c, :], xcb[:, 512:515], reads=[kxc + ('m',)], writes=[('hist', c)])

            def stB(c):
                if c >= 12:
                    return
                xcb = xc[c % 2]
                kxc = ('xc', c % 2)
                pc = G[2 + c % 2]
                S.mm([(pc[:], diag[:, c * 4 + i, :], xcb[:, i:i + 512], i == 0, i == 3) for i in range(4)],
                     reads=[kxc, kxc + ('m',)] + [('diag', c * 4 + i) for i in range(4)], writes=[('G', 2 + c % 2)])
                if c >= 8:
                    S.op('act', 'activation', vT[:, c - 8, :], pc[:], AF.Silu, reads=[('G', 2 + c % 2)], writes=[('vT', c - 8)])
                    return
                cb = c % 4
                S.op('act', 'activation', csL[cb][:], pc[:], AF.Silu, reads=[('G', 2 + c % 2)], writes=[('cs', cb)])
                S.op('dve', 'tensor_tensor', sqL[cb][:], csL[cb][:], csL[cb][:], ALU.mult, reads=[('cs', cb)], writes=[('sq', cb)])

            def stC(c):
                if c >= 8:
                    return
                cb = c % 4
                cs, sq, sd = csL[cb], sqL[cb], sdL[cb]
                S.mm([(G[4 + c % 2][:], onesb[:], sq[:], True, True)], reads=[('sq', cb), 'onesb'], writes=[('G', 4 + c % 2)])
                S.op('act', 'activation', sd[:], G[4 + c % 2][:], AF.Ln, bias=EPS, reads=[('G', 4 + c % 2)], writes=[('sd', cb)])
                S.op('act', 'activation', sd[:], sd[:], AF.Exp, scale=-0.5, reads=[('sd', cb)], writes=[('sd', cb)])
                dstT = qT if c < 4 else kT
                S.op('dve', 'scalar_tensor_tensor', dstT[:, c % 4, :], cs[:], (128.0 ** -0.5) if c < 4 else 1.0, sd[:],
                     ALU.mult, ALU.mult, reads=[('cs', cb), ('sd', cb)], writes=[('qT' if c < 4 else 'kT', c % 4)])

            for blk4 in range(4):
                cs4 = list(range(blk4 * 4, blk4 * 4 + 4))
                stA(cs4[0])
                for j_ in range(4):
                    if j_ + 1 < 4:
                        stA(cs4[j_ + 1])
                    stB(cs4[j_])
                for c_ in cs4:
                    stC(c_)
            qk = [('qT', h) for h in range(4)]
            kk = [('kT', h) for h in range(4)]
            vk = [('vT', h) for h in range(4)]
            if L3 < 4:
                continue
            def chunk_gen(ci, sl_):
                tsl = slice(ci * 128, (ci + 1) * 128)
                b_bc = bc3(btt[:, ci, :])
                bg_bc = bc3(bgt[:, ci, :])
                g_bc = bc3(gt[:, ci, :])
                gc_bc = bc3(gct[:, ci, :])
                Bs = Bb[sl_]
                kB = ('B0' if sl_ == 0 else ('tp', 77))
                P = [G[3 * sl_ + i] for i in range(3)]
                kP = [('G', 3 * sl_ + i) for i in range(3)]

                def p4(i):
                    return P[i][:].rearrange("p (h c) -> p h c", h=4)
                B0k = Bs[:, 0:512].rearrange("p (h c) -> p h c", h=4)
                B0v = Bs[:, 512:1024].rearrange("p (h c) -> p h c", h=4)
                T = bt16[sl_]
                Fq = ft32[sl_]

                def K(n):
                    return (n, sl_)
                S.tr([(Bs[:, h * 128:(h + 1) * 128], kT[:, h, tsl], identb[:]) for h in range(4)] +
                     [(Bs[:, 512 + h * 128:512 + (h + 1) * 128], vT[:, h, tsl], identb[:]) for h in range(4)],
                     reads=kk + vk + ['identb'], writes=[kB])
                S.op('dve', 'tensor_tensor', T['vb'][:], B0v, b_bc, ALU.mult, reads=[kB, 'bt'], writes=[K('vb')])
                S.op('dve', 'tensor_tensor', T['kbg'][:], B0k, bg_bc, ALU.mult, reads=[kB, 'bgt'], writes=[K('kbg')])
                S.op('act', 'copy', T['ktok'][:], B0k, reads=[kB], writes=[K('ktok')])
                yield
                S.op('act', 'copy', Fq['Gb'][:], g_bc, reads=['gt'], writes=[K('Gb')])
                for h in range(4):
                    S.op('act', 'activation', Fq['diagB'][:, h, :], identf, AF.Copy, scale=btt[:, ci, h:h + 1],
                         reads=['cst', 'bt'], writes=[K('diagB')])
                S.mm([(p4(0)[:, h, :], Fq['Gb'][:, h, :], Umat, True, True) for h in range(4)], reads=[K('Gb'), 'cst'], writes=[kP[0]])
                S.mm([(p4(1)[:, h, :], onesf[:], Fq['diagB'][:, h, :], True, True) for h in range(4)], reads=[K('diagB'), 'onesf'], writes=[kP[1]])
                yield
                S.op('dve', 'tensor_tensor', Fq['Dm'][:], p4(0), gc_bc, ALU.subtract, reads=[kP[0], 'gct'], writes=[K('Dm')])
                S.op('act', 'activation', Fq['egc'][:], p4(0), AF.Exp, reads=[kP[0]], writes=[K('egc')])
                S.op('pool', 'tensor_tensor', Fq['Dm'][:], Fq['Dm'][:], bch(maskneg), ALU.add, reads=[K('Dm'), 'cst'], writes=[K('Dm')])
                S.op('act', 'activation', Fq['DTm'][:], Fq['Dm'][:], AF.Exp, reads=[K('Dm')], writes=[K('DTm')])
                yield
                S.mm([(p4(2)[:, h, :], kT[:, h, tsl], kT[:, h, tsl], True, True) for h in range(4)], reads=kk, writes=[kP[2]])
                S.mm([(p4(0)[:, h, :], kT[:, h, tsl], qT[:, h, tsl], True, True) for h in range(4)], reads=kk + qk, writes=[kP[0]])
                yield
                S.op('dve', 'tensor_tensor', T['attnT'][:], p4(0), Fq['DTm'][:], ALU.mult, reads=[kP[0], K('DTm')], writes=[K('attnT')])
                S.op('dve', 'tensor_tensor', Fq['Gb'][:], p4(2), Fq['DTm'][:], ALU.mult, reads=[kP[2], K('DTm')], writes=[K('Gb')])
                S.op('dve', 'tensor_tensor', T['AT'][:], p4(1), Fq['Gb'][:], ALU.mult, reads=[kP[1], K('Gb')], writes=[K('AT')])
                S.op('pool', 'tensor_tensor', T['qdecT'][:], qT[:, :, tsl], Fq['egc'][:], ALU.mult, reads=qk + [K('egc')], writes=[K('qdecT')])
                for h in range(4):
                    S.op('act', 'activation', T['kdec'][:, h, :], T['ktok'][:, h, :], AF.Copy, scale=Fq['DTm'][:, h, 127:128],
                         reads=[K('ktok'), K('DTm')], writes=[K('kdec')])
                yield
                for l in range(7):
                    Ao = T['Ao%d' % (l % 3)]
                    ak = K('Ao%d' % (l % 3))
                    S.op('pool', 'tensor_tensor', Ao[:], T['AT'][:], bch(masks[:, l, :]), ALU.mult, reads=[K('AT'), 'masks'], writes=[ak])
                    if l == 0:
                        S.op('dve', 'tensor_tensor', T['TT'][:], bch(identb[:]), Ao[:], ALU.subtract, reads=['identb', ak], writes=[K('TT')])
                        S.tr([(Bs[:, h * 128:(h + 1) * 128], Ao[:, h, :], identb[:]) for h in range(4)], reads=[ak, 'identb'], writes=[kB])
                        S.op('dve', 'tensor_tensor', T['Tm'][:], bch(identb[:]), B0k, ALU.subtract, reads=['identb', kB], writes=[K('Tm')])
                        yield
                        continue
                    S.mm([(p4(0)[:, h, :], Ao[:, h, :], T['Tm'][:, h, :], True, True) for h in range(4)], reads=[ak, K('Tm')], writes=[kP[0]])
                    S.op('act', 'copy', T['Xs'][:], p4(0), reads=[kP[0]], writes=[K('Xs')])
                    yield
                    if l < 6:
                        S.mm([(p4(1)[:, h, :], T['TT'][:, h, :], T['Xs'][:, h, :], True, True) for h in range(4)], reads=[K('TT'), K('Xs')], writes=[kP[1]])
                    S.mm([(p4(2)[:, h, :], T['Xs'][:, h, :], T['TT'][:, h, :], True, True) for h in range(4)], reads=[K('TT'), K('Xs')], writes=[kP[2]])
                    if l < 6:
                        S.op('dve', 'tensor_tensor', T['Tm'][:], T['Tm'][:], p4(1), ALU.subtract, reads=[K('Tm'), kP[1]], writes=[K('Tm')])
                    S.op('dve', 'tensor_tensor', T['TT'][:], T['TT'][:], p4(2), ALU.subtract, reads=[K('TT'), kP[2]], writes=[K('TT')])
                    yield
                S.mm([(p4(0)[:, h, :], T['TT'][:, h, :], T['vb'][:, h, :], True, True) for h in range(4)], reads=[K('TT'), K('vb')], writes=[kP[0]])
                S.mm([(p4(1)[:, h, :], T['kbg'][:, h, :], T['TT'][:, h, :], True, True) for h in range(4)], reads=[K('TT'), K('kbg')], writes=[kP[1]])
                S.op('act', 'copy', Fq['u'][:], p4(0), reads=[kP[0]], writes=[K('u')])
                S.op('dve', 'tensor_copy', T['wT'][:], p4(1), reads=[kP[1]], writes=[K('wT')])
                yield
                S.mm([(p4(2)[:, h, :], T['wT'][:, h, :], Sb[:, h, :], True, True) for h in range(4)], reads=[K('wT'), 'Sb'], writes=[kP[2]])
                S.op('dve', 'tensor_tensor', T['vnew'][:], Fq['u'][:], p4(2), ALU.subtract, reads=[K('u'), kP[2]], writes=[K('vnew')])
                it = []
                for h in range(4):
                    it += [(p4(0)[:, h, :], Sb[:, h, :], T['qdecT'][:, h, :], True, False),
                           (p4(0)[:, h, :], T['vnew'][:, h, :], T['attnT'][:, h, :], False, True)]
                S.mm(it, reads=['Sb', K('qdecT'), K('vnew'), K('attnT')], writes=[kP[0]])
                S.mm([(p4(1)[:, h, :], T['kdec'][:, h, :], T['vnew'][:, h, :], True, True) for h in range(4)], reads=[K('kdec'), K('vnew')], writes=[kP[1]])
                S.op('pool', 'tensor_tensor', Fq['Dm'][:], Sf[:], Fq['egc'][:, :, 127:128].to_broadcast([128, 4, 128]), ALU.mult,
                     reads=['Sf', K('egc')], writes=[K('Dm')])
                S.op('dve', 'tensor_tensor', Sf[:], Fq['Dm'][:], p4(1), ALU.add, reads=[K('Dm'), kP[1]], writes=['Sf'])
                S.op('act', 'copy', Sb[:], Sf[:], reads=['Sf'], writes=['Sb'])
                S.op('act', 'activation', T['Xs'][:], p4(0), AF.Square, reads=[kP[0]], writes=[K('Xs')])
                S.mm([(P[2][:], onesb[:], T['Xs'][:].rearrange("p h c -> p (h c)"), True, True)], reads=[K('Xs'), 'onesb'], writes=[kP[2]])
                S.op('act', 'activation', Fq['DTm'][:], p4(2), AF.Ln, bias=EPS, scale=1.0 / 128, reads=[kP[2]], writes=[K('DTm')])
                S.op('act', 'activation', Fq['DTm'][:], Fq['DTm'][:], AF.Exp, scale=-0.5, reads=[K('DTm')], writes=[K('DTm')])
                S.op('dve', 'scalar_tensor_tensor', Fq['u'][:], p4(0), hpar_t[:, 8:9], Fq['DTm'][:], ALU.mult, ALU.mult,
                     reads=[kP[0], K('DTm'), 'hpar'], writes=[K('u')])
                S.op('dve', 'tensor_tensor', oaT[:, :, tsl], Fq['u'][:], sgate[:, :, tsl], ALU.mult,
                     reads=[K('u')] + [('sgate', h) for h in range(4)], writes=[('oaT', ci)])
                if dbg_t:
                    S.dma(dbg_t['oa'][:, :, tg * 512 + ci * 128:tg * 512 + (ci + 1) * 128], oaT[:, :, tsl],
                          reads=[('oaT', ci)], writes=[('dbgoa', ci, tg)])
                yield
                t = tg * 4 + ci
                for half in range(2):
                    it = []
                    for fc in range(8):
                        src = oaT[:, fc, tsl] if fc < 4 else obT[:, fc - 4, tsl]
                        it.append((P[half][:], src, Wo[:, fc, half * 512:(half + 1) * 512], fc == 0, fc == 7))
                    S.mm(it, reads=[('oaT', ci), 'obT'] + WoK, writes=[kP[half]])
                post_norm_residual(P[0][:], P[1][:], [kP[0], kP[1]], x[t * 128:(t + 1) * 128, :], [],
                                   gpb, out[t * 128:(t + 1) * 128, :], [('out', t)], xt[sl_], tmp[0], sl_, tslot=0)
                yield

            for pair in ((0, 1), (2, 3)):
                gens = [chunk_gen(pair[0], 0), chunk_gen(pair[1], 1)]
                live = list(gens)
                while live:
                    for g_ in list(live):
                        try:
                            next(g_)
                        except StopIteration:
                            live.remove(g_)
        S.barrier()


def phase5(nc, S, sbt, pst, C):
    w_gate, w_up, w_down, gpost, out = C['w_gate'], C['w_up'], C['w_down'], C['gpost'], C['out']
    load_w, wk, norm_T, post_norm_residual = C['load_w'], C['wk'], C['norm_T'], C['post_norm_residual']
    NG = C.get('ngroups', 8)
    with ExitStack() as s5:
        Wga = sbt(s5, "Wga", [128, 8, DFF], BF16)
        Wup = sbt(s5, "Wup", [128, 8, DFF], BF16)
        Wdn = sbt(s5, "Wdn", [128, 22, DM], BF16)
        gpb = sbt(s5, "gpb5", [128, DM], F32)
        xt = [sbt(s5, "xt5_%d" % i, [128, DM], F32) for i in range(2)]
        xn = [sbt(s5, "xn5_%d" % i, [128, DM], BF16) for i in range(2)]
        tmp = [sbt(s5, "tmp5_%d" % i, [128, DM], F32) for i in range(2)]
        h2TL = [sbt(s5, "h2T%d" % i, [128, 8, 512], BF16) for i in range(2)]
        actT = sbt(s5, "actT", [128, 22, 512], BF16)
        sg = [sbt(s5, "sg%d" % i, [128, 512], BF16) for i in range(2)]
        F = [pst(s5, "F%d" % i, [128, 512], F32) for i in range(7)]
        TP = pst(s5, "TP5", [128, 1024], BF16)
        print("phase5 sbuf bytes remaining", nc.sbuf_bytes_remaining)
        S.dma(gpb[:], gpost[1:2, :].partition_broadcast(128), writes=['gp'])
        for pc_ in range(4):
            load_w(Wga[:, :, pc_ * 704:(pc_ + 1) * 704], ('Wga', pc_), w_gate, pc_ * 704, 704, 8)
            load_w(Wup[:, :, pc_ * 704:(pc_ + 1) * 704], ('Wup', pc_), w_up, pc_ * 704, 704, 8)
        for pc_ in range(11):
            load_w(Wdn[:, pc_ * 2:(pc_ + 1) * 2, :], ('Wdn', pc_), w_down, 0, DM, 2, r0=pc_ * 256)
        WdnK = [k for pc_ in range(11) for k in wk(('Wdn', pc_))]
        def prep1(tg, i):
            t = tg * 4 + i
            norm_T(out[t * 128:(t + 1) * 128, :], h2TL[tg % 2][:, :, i * 128:(i + 1) * 128],
                   xt[i % 2], xn[i % 2], TP, i % 2, [('out', t)], [('h2T', tg % 2, i)], tpkey=('tp', 9), g0=8)

        def prep(tg):
            for i in range(4):
                prep1(tg, i)
        prep(0)
        for tg in range(NG):
            tiles = [tg * 4 + i for i in range(4)]
            h2T = h2TL[tg % 2]
            hk = [('h2T', tg % 2, i) for i in range(4)]
            for fc in range(22):
                pg = F[(2 * fc) % 4]
                pu = F[(2 * fc + 1) % 4]
                kg = ('F', (2 * fc) % 4)
                ku = ('F', (2 * fc + 1) % 4)
                S.mm([(pg[:], Wga[:, kc, fc * 128:(fc + 1) * 128], h2T[:, kc, :], kc == 0, kc == 7) for kc in range(8)],
                     reads=hk + wk(('Wga', (fc * 128) // 704)) + wk(('Wga', (fc * 128 + 127) // 704)), writes=[kg])
                S.mm([(pu[:], Wup[:, kc, fc * 128:(fc + 1) * 128], h2T[:, kc, :], kc == 0, kc == 7) for kc in range(8)],
                     reads=hk + wk(('Wup', (fc * 128) // 704)) + wk(('Wup', (fc * 128 + 127) // 704)), writes=[ku])
                S.op('act', 'activation', sg[fc % 2][:], pg[:], AF.Silu, reads=[kg], writes=[('sg', fc % 2)])
                S.op('dve', 'tensor_tensor', actT[:, fc, :], pu[:], sg[fc % 2][:], ALU.mult,
                     reads=[ku, ('sg', fc % 2)], writes=[('actT', fc)])
            ak = [('actT', fc) for fc in range(22)]
            for i, t in enumerate(tiles):
                mb = (4, 5) if i % 2 == 0 else (6, 0)
                for half in range(2):
                    S.mm([(F[mb[half]][:], actT[:, fc, i * 128:(i + 1) * 128], Wdn[:, fc, half * 512:(half + 1) * 512],
                           fc == 0, fc == 21) for fc in range(22)], reads=ak + WdnK, writes=[('F', mb[half])])
                if tg + 1 < NG:
                    prep1(tg + 1, i)
                post_norm_residual(F[mb[0]][:], F[mb[1]][:], [('F', mb[0]), ('F', mb[1])], out[t * 128:(t + 1) * 128, :],
                                   [('out', t)], gpb, out[t * 128:(t + 1) * 128, :], [('out', t)], xt[i % 2], tmp[i % 2], i % 2)
        S.barrier()


def phase2(nc, S, sbt, pst, hT, w_in, biasT, scr, dbg_t, identb, load_w, wk):
    with ExitStack() as s2:
        QT2 = sbt(s2, "QT2", [128, 2, SEQ], BF16)
        QTA = QT2[:, 0, :]
        QTB = QT2[:, 1, :]
        tmpz = sbt(s2, "tmpz", [128, 512], F32)
        nzs = [sbt(s2, "nzs%d" % i, [128, 2, 128], F32) for i in range(2)]
        KT = sbt(s2, "KT", [128, SEQ], BF16)
        VT = sbt(s2, "VT", [128, SEQ], BF16)
        V3 = [sbt(s2, "V3_%d" % i, [128, 32, 256], BF16) for i in range(2)]
        acc = sbt(s2, "acc", [128, 2, SEQ], F32)
        Et = sbt(s2, "Et", [128, 3, 512], BF16)
        ebias = sbt(s2, "ebias", [128, 512], F32)
        pt = [sbt(s2, "pt%d" % i, [128, 512], BF16) for i in range(5)]
        wqkvL = [sbt(s2, "wqkv%d" % i, [128, 8, 384], BF16) for i in range(2)]
        onesAB = sbt(s2, "onesAB", [128, 256], BF16)
        obuf = [sbt(s2, "obuf%d" % i, [128, 512], BF16) for i in range(2)]
        sp = [pst(s2, "sp%d" % i, [128, 512], F32) for i in range(5)]
        pj = [sp[3], sp[4]]
        nz = [pst(s2, "nz%d" % i, [128, 256], F32)[:] for i in range(2)]
        vtp_all = pst(s2, "vtp", [128, 512], BF16)
        vtp = [vtp_all[:], vtp_all[:]]
        S.op('pool', 'memset', onesAB[:], 0.0, writes=['onesAB'])
        S.op('pool', 'memset', onesAB[:, 0:64], 1.0, reads=['onesAB'], writes=['onesAB'])
        S.op('pool', 'memset', onesAB[:, 192:256], 1.0, reads=['onesAB'], writes=['onesAB'])
        for i in range(2):
            S.op('pool', 'memset', V3[i][:, :, 64:192], 1.0, writes=[('V3z', i)])
        print("phase2 sbuf bytes remaining", nc.sbuf_bytes_remaining)
        S.op('pool', 'memset', QTA[64:128, :], 0.0, writes=['QTz'])
        S.op('pool', 'memset', QTB[0:64, :], 0.0, writes=['QTzb'])
        pcount = 0
        ucount = 0
        zcount = 0
        import os
        LIM = int(os.environ.get("P2LIM", "99"))
        def ldw(hq):
            for j, c0 in enumerate((2056 + hq * 128, 2568 + hq * 128, 3080 + hq * 128)):
                load_w(wqkvL[hq % 2][:, :, j * 128:(j + 1) * 128], ('wqkv', hq % 2, j), w_in, c0, 128, 8)

        def proj(hq):
            wqkv = wqkvL[hq % 2]
            for tg in range(8):
                for j in range(3):
                    b = (tg * 3 + j) % 2
                    S.mm([(pj[b][:], wqkv[:, kc, j * 128:(j + 1) * 128], hT[:, kc, tg * 512:(tg + 1) * 512],
                           kc == 0, kc == 7) for kc in range(8)],
                         reads=wk(('wqkv', hq % 2, j)) + [('hT', tg * 4 + i) for i in range(4)], writes=[('sp', 3 + b)])
                    sl = slice(tg * 512, (tg + 1) * 512)
                    if j == 0:
                        S.op('act', 'mul', QTA[0:64, sl], pj[b][0:64, :], 0.125, reads=[('sp', 3 + b), 'QTz'], writes=[('QT', tg)])
                        S.op('act', 'mul', QTB[64:128, sl], pj[b][64:128, :], 0.125, reads=[('sp', 3 + b), 'QTzb'], writes=[('QTb', tg)])
                    elif j == 1:
                        S.op('act', 'copy', KT[:, sl], pj[b][:], reads=[('sp', 3 + b)], writes=[('KT', tg)])
                    else:
                        S.op('act', 'copy', VT[:, sl], pj[b][:], reads=[('sp', 3 + b)], writes=[('VT', tg)])

        NHP = 4 if LIM >= 9 else 1
        ldw(0)
        proj(0)
        for hp in range(NHP):
            if hp + 1 < NHP:
                ldw(hp + 1)
            if LIM < 2:
                continue
            for p in range(3):
                S.dma(ebias[:], biasT[hp, :, p * 512:(p + 1) * 512], writes=['ebias'])
                S.op('act', 'activation', Et[:, p, :], ebias[:], AF.Exp, reads=['ebias'], writes=[('E', p)])
            if LIM < 3:
                continue
            PATS = (1, 4, 16)

            def mk(d):
                nb_ = 32 // d

                def tok_(blk):
                    r, n = divmod(blk, nb_)
                    s_ = n * 128 * d + r
                    return slice(s_, s_ + 127 * d + 1, d)

                def tgs_(blk):
                    r, n = divmod(blk, nb_)
                    s_ = n * 128 * d + r
                    return list(range(s_ // 512, (s_ + 127 * d) // 512 + 1))
                return nb_, tok_, tgs_

            def emit_V(p_, vslot_, g_lo, g_hi):
                nb_, tok_, tgs_ = mk(PATS[p_])
                vb_ = V3[vslot_]
                for g4 in range(g_lo, g_hi):
                    rk = set()
                    for i in range(4):
                        rk.update(tgs_(g4 * 4 + i))
                    S.tr([(vtp[0][:, i * 128:(i + 1) * 128], VT[:, tok_(g4 * 4 + i)], identb[:]) for i in range(4)],
                         reads=[('VT', t) for t in sorted(rk)] + ['identb'], writes=[('vtp', 0)])
                    srcv = vtp[0].rearrange("p (b c) -> p b c", b=4)
                    S.op('dve', 'tensor_copy', vb_[:, g4 * 4:(g4 + 1) * 4, 0:64], srcv[:, :, 0:64],
                         reads=[('vtp', 0), ('V3z', vslot_)], writes=[('V3', vslot_, g4)])
                    S.op('dve', 'tensor_copy', vb_[:, g4 * 4:(g4 + 1) * 4, 192:256], srcv[:, :, 64:128],
                         reads=[('vtp', 0), ('V3z', vslot_)], writes=[('V3b', vslot_, g4)])

            emit_V(0, pcount % 2, 0, 8)
            for p, d in enumerate(PATS):
                nb, tok, tgs = mk(d)
                vslot = pcount % 2
                pcount += 1
                vb = V3[vslot]

                if LIM < 4:
                    continue

                def emit_S(blk, u):
                    r, n = divmod(blk, nb)
                    s = u % 5
                    qs = tok(blk)
                    items = [(sp[s][:, 0:256], KT[:, qs], QT2[:, :, qs], True, True)]
                    rk = set(tgs(blk))
                    if n > 0:
                        ks = tok(blk - 1)
                        rk.update(tgs(blk - 1))
                        items += [(sp[s][:, 256:512], KT[:, ks], QT2[:, :, qs], True, True)]
                    VAR = os.environ.get("P2VAR", "")
                    if VAR == "a":
                        items = [it for k, it in enumerate(items) if k % 2 == 0]
                    if VAR == "b":
                        items = [it for k, it in enumerate(items) if k % 2 == 1]
                    if VAR == "c":
                        for it in items:
                            S.mm([it], reads=[('QT', t) for t in sorted(rk)] + [('QTb', t) for t in sorted(rk)] + [('KT', t) for t in sorted(rk)], writes=[('sp', s)])
                        items = []
                    if items:
                      S.mm(items, reads=[('QT', t) for t in sorted(rk)] + [('QTb', t) for t in sorted(rk)] + [('KT', t) for t in sorted(rk)],
                         writes=[('sp', s)])
                    wd = 512 if n > 0 else 256
                    S.op('act', 'activation', pt[s][:, 0:wd], sp[s][:, 0:wd], AF.Exp,
                         reads=[('sp', s)], writes=[('pt', s)])
                    S.op('dve', 'tensor_tensor', pt[s][:, 0:wd], pt[s][:, 0:wd], Et[:, p, 0:wd], ALU.mult,
                         reads=[('pt', s), ('E', p)], writes=[('pt', s)])

                def emit_PV(blk, u, zc):
                    SUB = int(os.environ.get("P2SUB", "9"))
                    if SUB < 2:
                        return
                    r, n = divmod(blk, nb)
                    s = u % 5
                    zs = zc % 2
                    P_ = pt[s]
                    itN = [(nz[zs][:, 0:128], vb[:, blk, 0:128], P_[:, 0:128], True, n == 0)]
                    itZ = [(nz[zs][:, 128:256], vb[:, blk, 128:256], P_[:, 128:256], True, n == 0)]
                    rk = [('pt', s), ('V3', vslot, blk // 4), ('V3b', vslot, blk // 4), ('V3z', vslot)]
                    if n > 0:
                        itN += [(nz[zs][:, 0:128], vb[:, blk - 1, 0:128], P_[:, 256:384], False, True)]
                        itZ += [(nz[zs][:, 128:256], vb[:, blk - 1, 128:256], P_[:, 384:512], False, True)]
                        rk += [('V3', vslot, (blk - 1) // 4), ('V3b', vslot, (blk - 1) // 4)]
                    S.mm(itN + itZ, reads=rk, writes=[('nz', zs)])
                    if SUB < 3:
                        return
                    qs = tok(blk)
                    accv = acc[:, :, qs]
                    nzv = nz[zs].rearrange("p (a q) -> p a q", a=2)
                    akeys = [('acc', t) for t in range(qs.start // 128, (qs.start + 127 * d) // 128 + 1)]
                    odd = (blk % 2 == 1)
                    if p == 0:
                        if odd:
                            S.op('act', 'copy', accv, nzv, reads=[('nz', zs)], writes=akeys)
                        else:
                            S.op('dve', 'tensor_copy', accv, nzv, reads=[('nz', zs)], writes=akeys)
                    elif odd:
                        k_ = (blk // 2) % 2
                        S.op('act', 'copy', nzs[k_][:], nzv, reads=[('nz', zs)], writes=[('nzs', k_)])
                        S.op('pool', 'tensor_tensor', accv, nzs[k_][:], accv, ALU.add,
                             reads=[('nzs', k_)] + akeys, writes=akeys)
                    else:
                        S.op('dve', 'tensor_tensor', accv, nzv, accv, ALU.add,
                             reads=[('nz', zs)] + akeys, writes=akeys)

                LA = 4
                for b_ in range(LA):
                    emit_S(b_, ucount + b_)
                for blk in range(32):
                    if p + 1 < 3 and blk in (8, 12, 16, 20):
                        g0_ = (blk - 8) // 2
                        emit_V(p + 1, pcount % 2, g0_, g0_ + 2)
                    if blk + LA < 32:
                        emit_S(blk + LA, ucount + blk + LA)
                    emit_PV(blk, ucount + blk, zcount)
                    zcount += 1
                ucount += 32
            if LIM < 6:
                continue
            if hp + 1 < NHP:
                proj(hp + 1)
            for tg in range(8):
                sl = slice(tg * 512, (tg + 1) * 512)
                ak = [('acc', tg * 4 + i) for i in range(4)]
                ob = obuf[tg % 2]
                S.op('act', 'activation', tmpz[0:64, :], acc[64:128, 0, sl], AF.Ln, reads=ak, writes=['tmpza'])
                S.op('act', 'activation', tmpz[64:128, :], acc[0:64, 1, sl], AF.Ln, reads=ak, writes=['tmpzb'])
                S.op('act', 'activation', tmpz[:], tmpz[:], AF.Exp, scale=-1.0, reads=['tmpza', 'tmpzb'], writes=['tmpza', 'tmpzb'])
                S.op('dve', 'tensor_tensor', ob[0:64, :], acc[0:64, 0, sl], tmpz[0:64, :], ALU.mult,
                     reads=ak + ['tmpza'], writes=[('obuf', tg % 2)])
                S.op('pool', 'tensor_tensor', ob[64:128, :], acc[64:128, 1, sl], tmpz[64:128, :], ALU.mult,
                     reads=ak + ['tmpzb'], writes=[('obuf', tg % 2, 'b')])
                S.dma(scr[:, hp, sl], ob[:], reads=[('obuf', tg % 2), ('obuf', tg % 2, 'b')], writes=[('scr', tg)], eng='pool')
                if dbg_t:
                    S.dma(dbg_t['ob'][:, hp, sl], ob[:], reads=[('obuf', tg % 2), ('obuf', tg % 2, 'b')], writes=[('dbgob', hp, tg)])
        S.barrier()


def _t5_bucket(dist):
    d = np.maximum(dist, 1).astype(np.float32)
    log_b = 16 + (np.log(d / np.float32(16)) / np.float32(math.log(2048 / 16)) * np.float32(16)).astype(np.int32)
    return np.where(dist < 16, dist, np.minimum(log_b, 31))


def _host_layouts(inp):
    f32 = np.float32
    gpre = np.zeros((128, 16), f32)
    gpre[:, 0:8] = inp['g_mix_pre'][0].reshape(8, 128).T
    gpre[:, 8:16] = inp['g_ffn_pre'][0].reshape(8, 128).T
    gpost = np.stack([inp['g_mix_post'][0], inp['g_ffn_post'][0]]).astype(f32)
    convw = np.ascontiguousarray(inp['conv_w'][0].reshape(4, 12, 128).transpose(2, 1, 0)).reshape(128, 48).astype(f32)
    hpar = np.zeros((128, 16), f32)
    hpar[:, 0:4] = inp['a_log'][0][None, :]
    hpar[:, 4:8] = inp['dt_bias'][0][None, :]
    hpar[:, 8] = inp['onorm_g'][0]
    rb = inp['rel_bias'].astype(f32)
    jj = np.arange(128)[:, None]
    ii = np.arange(128)[None, :]
    biasT = np.full((4, 128, 3, 4, 128), NEG, f32)
    for p, dil in enumerate((1, 4, 16)):
        bs = rb[_t5_bucket(np.arange(129) * dil)]
        rel_c = ii - jj
        rel_p = ii + 128 - jj
        for h in range(8):
            cur = np.where(rel_c >= 0, bs[np.clip(rel_c, 0, 128), h], f32(NEG))
            prv = np.where(rel_p <= 128, bs[np.clip(rel_p, 0, 128), h], f32(NEG))
            biasT[h // 2, :, p, (h % 2), :] = cur
            biasT[h // 2, :, p, 2 + (h % 2), :] = prv
    biasT = biasT.reshape(4, 128, 1536)
    consts = np.zeros((128, 1280), f32)
    consts[:, 0:128] = np.eye(128, dtype=f32)
    consts[:, 128:256] = (jj <= ii).astype(f32)
    consts[:, 256:384] = np.where(ii >= jj, 0.0, NEG)
    for l in range(7):
        b = 1 << l
        m = ((ii // (2 * b)) == (jj // (2 * b))) & ((ii % (2 * b)) >= b) & ((jj % (2 * b)) < b)
        consts[:, 384 + l * 128:384 + (l + 1) * 128] = m.astype(f32)
    return dict(gpre=gpre, gpost=gpost, convw=convw, hpar=hpar, biasT=biasT, consts=consts)


def kernel(**inputs):
    inp = {k: np.asarray(v) for k, v in inputs.items()}
    lay = _host_layouts(inp)
    nc = build_nc()
    shared = dict(w_in=np.ascontiguousarray(inp['w_in'][0]), w_out=np.ascontiguousarray(inp['w_out'][0]),
                  w_gate=np.ascontiguousarray(inp['w_gate'][0]), w_up=np.ascontiguousarray(inp['w_up'][0]),
                  w_down=np.ascontiguousarray(inp['w_down'][0]), **lay)
    in_maps = [dict(shared, x=np.ascontiguousarray(inp['x'][b])) for b in range(8)]
    res = run_bass_kernel_spmd(nc, in_maps, core_ids=list(range(8)))
    return np.stack([np.asarray(r["out"]) for r in res.results], axis=0).astype(np.float32)
```

```python
import math
import numpy as np
from contextlib import ExitStack
import concourse.bass as bass
import concourse.mybir as mybir
from concourse.bass_utils import run_bass_kernel_spmd

F32 = mybir.dt.float32
BF16 = mybir.dt.bfloat16
AF = mybir.ActivationFunctionType
ALU = mybir.AluOpType

NDS = 24
NDP = 8
SEQ = 4096
DM = 1024
DFF = 2816
NEG = -30000.0
EPS = 1e-6


_PSUM_KEYS = ('G', 'pj', 'sp', 'nz', 'vtp', 'tp', 'B0', 'F')


def _is_psum_key(k):
    return k == 'B0' or (isinstance(k, tuple) and k[0] in _PSUM_KEYS)


class Sched:
    def __init__(self, nc, es):
        self.nc = nc
        self.engs = {'pe': nc.tensor, 'act': nc.scalar, 'dve': nc.vector,
                     'pool': nc.gpsimd, 'sp': nc.sync}
        self.sem = {k: es.enter_context(nc.semaphore('s_' + k)) for k in self.engs}
        self.cnt = {k: 0 for k in self.engs}
        self.waited = {k: {} for k in self.engs}
        self.dsem = [es.enter_context(nc.semaphore('d%d' % i)) for i in range(NDS + NDP)]
        self.dval = [0] * (NDS + NDP)
        self.dnext = 0
        self.pnext = 0
        self.last_w = {}
        self.readers = {}
        self.prog = {k: [] for k in self.engs}
        self.nins = {k: 0 for k in self.engs}
        self.pend = {k: [] for k in self.engs}
        self.hist = {}

    def _semof(self, sk):
        return self.sem[sk] if isinstance(sk, str) else self.dsem[sk[1]]

    def wait(self, eng, tok):
        sk, val = tok
        if sk == eng and eng == 'pe':
            return
        if self.waited[eng].get(sk, 0) >= val:
            return
        self.pend[eng].append((self._semof(sk), val))
        self.waited[eng][sk] = val
        for k2, v2 in self.hist.get(tok, {}).items():
            if self.waited[eng].get(k2, 0) < v2:
                self.waited[eng][k2] = v2

    def _take(self, eng):
        p = self.pend[eng]
        self.pend[eng] = []
        for sem, val in p[:-1]:
            self.prog[eng].append(lambda e, sem=sem, val=val: e.wait_ge(sem, val))
        return p[-1] if p else None

    def flush(self, eng):
        for sem, val in self.pend[eng]:
            self.prog[eng].append(lambda e, sem=sem, val=val: e.wait_ge(sem, val))
        self.pend[eng] = []

    def _deps(self, eng, reads, writes):
        deps = {}

        def add(tok, war=False):
            sk, val = tok
            if war and sk == eng and eng == 'pe':
                return
            if deps.get(sk, 0) < val:
                deps[sk] = val
        for k in reads:
            if k in self.last_w:
                add(self.last_w[k])
            if _is_psum_key(k):
                for sk, val in self.readers.get(k, {}).items():
                    if sk != eng:
                        add((sk, val))
        for k in writes:
            if k in self.last_w:
                add(self.last_w[k], war=True)
            for sk, val in self.readers.get(k, {}).items():
                add((sk, val), war=True)
        for sk, val in deps.items():
            self.wait(eng, (sk, val))

    def _record(self, tok, reads, writes, issuer=None):
        sk, val = tok
        src = sk if isinstance(sk, str) else issuer
        h = dict(self.waited[src])
        if isinstance(sk, str) and val > 1:
            h[sk] = max(h.get(sk, 0), val - 1)
        self.hist[tok] = h
        for k in reads:
            self.readers.setdefault(k, {})[sk] = val
        for k in writes:
            self.last_w[k] = tok
            self.readers[k] = {}

    def op(self, eng, meth, *args, reads=(), writes=(), **kw):
        self._deps(eng, reads, writes)
        sem = self.sem[eng]
        w = self._take(eng)

        def run(e, meth=meth, args=args, kw=kw, sem=sem, w=w):
            ins = getattr(e, meth)(*args, **kw)
            if w is not None:
                ins._wait_ge(w[0], w[1])
            ins.then_inc(sem, 1)
        self.prog[eng].append(run)
        self.cnt[eng] += 1
        self.nins[eng] += 1
        self._record((eng, self.cnt[eng]), reads, writes)

    def mm(self, items, reads=(), writes=()):
        self._deps('pe', reads, writes)
        sem = self.sem['pe']
        items = list(items)
        w = self._take('pe')

        def run(e, items=items, sem=sem, w=w):
            ins = None
            for k, it in enumerate(items):
                ins = e.matmul(it[0], it[1], it[2], start=it[3], stop=it[4])
                if k == 0 and w is not None:
                    ins._wait_ge(w[0], w[1])
            ins.then_inc(sem, 1)
        self.prog['pe'].append(run)
        self.cnt['pe'] += 1
        self.nins['pe'] += len(items)
        self._record(('pe', self.cnt['pe']), reads, writes)

    def tr(self, items, reads=(), writes=()):
        self._deps('pe', reads, writes)
        sem = self.sem['pe']
        items = list(items)
        w = self._take('pe')

        def run(e, items=items, sem=sem, w=w):
            ins = None
            for k, it in enumerate(items):
                ins = e.transpose(it[0], it[1], it[2])
                if k == 0 and w is not None:
                    ins._wait_ge(w[0], w[1])
            ins.then_inc(sem, 1)
        self.prog['pe'].append(run)
        self.cnt['pe'] += 1
        self.nins['pe'] += len(items)
        self._record(('pe', self.cnt['pe']), reads, writes)

    def dma(self, out, in_, reads=(), writes=(), eng='sp', **kw):
        self._deps(eng, reads, writes)
        if eng == 'pool':
            i = NDS + self.pnext
            self.pnext = (self.pnext + 1) % NDP
        else:
            i = self.dnext
            self.dnext = (self.dnext + 1) % NDS
        if self.dval[i] > 0:
            self.wait(eng, (('d', i), self.dval[i]))
        dsem = self.dsem[i]
        self.flush(eng)
        self.prog[eng].append(lambda e, out=out, in_=in_, kw=kw, dsem=dsem:
                              e.dma_start(out=out, in_=in_, **kw).then_inc(dsem, 16))
        self.dval[i] += 16
        self.nins[eng] += 1
        self._record((('d', i), self.dval[i]), reads, writes, issuer=eng)

    def barrier(self):
        for e in self.engs:
            for o in self.engs:
                if o != e and self.cnt[o] > 0:
                    self.wait(e, (o, self.cnt[o]))
            for i in range(NDS + NDP):
                if self.dval[i] > 0:
                    self.wait(e, (('d', i), self.dval[i]))
        for e in self.engs:
            self.flush(e)
        self.last_w = {}
        self.readers = {}

    def emit(self):
        for e in self.engs:
            self.flush(e)
        with self.nc.Block() as block:
            for name, reg in (('pe', block.tensor), ('act', block.scalar), ('dve', block.vector),
                              ('pool', block.gpsimd), ('sp', block.sync)):
                prog = self.prog[name]

                def body(e, prog=prog):
                    for f in prog:
                        f(e)
                reg(body)


def build_nc(stop=9, dbg=False, ngroups=8):
    nc = bass.Bass("TRN2", target_bir_lowering=False)

    def D(name, shape, dt=F32, kind="ExternalInput"):
        return nc.dram_tensor(name, shape, dt, kind=kind).ap()
    x = D("x", [SEQ, DM])
    w_in = D("w_in", [DM, 3592])
    w_out = D("w_out", [DM, DM])
    w_gate = D("w_gate", [DM, DFF])
    w_up = D("w_up", [DM, DFF])
    w_down = D("w_down", [DFF, DM])
    gpre = D("gpre", [128, 16])
    gpost = D("gpost", [2, DM])
    convw = D("convw", [128, 48])
    hpar = D("hpar", [128, 16])
    biasT = D("biasT", [4, 128, 1536])
    consts = D("consts", [128, 1280])
    out = D("out", [SEQ, DM], kind="ExternalOutput")
    scr = D("scr_ob", [128, 4, SEQ], BF16, kind="Internal")
    dbg_t = {}
    if dbg:
        dbg_t['ob'] = D("dbg_ob", [128, 4, SEQ], BF16, kind="ExternalOutput")
        dbg_t['oa'] = D("dbg_oa", [128, 4, SEQ], BF16, kind="ExternalOutput")

    es = ExitStack()
    with es:
        S = Sched(nc, es)

        def sbt(st, name, shape, dt):
            return st.enter_context(nc.sbuf_tensor(name, shape, dt))

        def pst(st, name, shape, dt):
            return st.enter_context(nc.psum_tensor(name, shape, dt))

        cst = sbt(es, "cst", [128, 1280], F32)
        identf = cst[:, 0:128]
        Umat = cst[:, 128:256]
        maskneg = cst[:, 256:384]
        identb = sbt(es, "identb", [128, 128], BF16)
        onesb = sbt(es, "onesb", [128, 128], BF16)
        onesf = sbt(es, "onesf", [128, 128], F32)
        gpre_t = sbt(es, "gpre_t", [128, 16], F32)
        hpar_t = sbt(es, "hpar_t", [128, 16], F32)
        st = sbt(es, "stats", [128, 64], F32)
        junk = sbt(es, "junk", [128, 1024], BF16)
        S.dma(cst[:], consts, writes=['cst'])
        S.dma(gpre_t[:], gpre, writes=['gpre'])
        S.dma(hpar_t[:], hpar, writes=['hpar'])
        S.op('dve', 'tensor_copy', identb[:], identf, reads=['cst'], writes=['identb'])
        S.op('pool', 'memset', onesb[:], 1.0, writes=['onesb'])
        S.op('pool', 'memset', onesf[:], 1.0, writes=['onesf'])

        nst = [0, 0]

        def norm_T(src, dst, xt_t, xn_t, tp_t, slot, rkeys, wkeys, xkey=None, tpkey=None, g0=0):
            i = nst[0] % 8
            nst[0] += 1
            c = 4 * i
            kx = ('xt', slot) if xkey is None else xkey
            ktp = ('tp', slot) if tpkey is None else tpkey
            S.dma(xt_t[:], src, reads=rkeys, writes=[kx])
            S.op('act', 'activation', junk[:], xt_t[:], AF.Square, accum_out=st[:, c:c + 1],
                 reads=[kx], writes=['junk', ('st', i)])
            S.op('act', 'activation', st[:, c + 1:c + 2], st[:, c:c + 1], AF.Ln, bias=EPS, scale=1.0 / DM,
                 reads=[('st', i)], writes=[('st', i)])
            S.op('act', 'activation', st[:, c + 2:c + 3], st[:, c + 1:c + 2], AF.Exp, scale=-0.5,
                 reads=[('st', i)], writes=[('st', i)])
            S.op('act', 'activation', xn_t[:], xt_t[:], AF.Copy, scale=st[:, c + 2:c + 3],
                 reads=[kx, ('st', i)], writes=[('xn', slot)])
            S.tr([(tp_t[:, k * 128:(k + 1) * 128], xn_t[:, k * 128:(k + 1) * 128], identb[:]) for k in range(8)],
                 reads=[('xn', slot), 'identb'], writes=[ktp])
            S.op('dve', 'tensor_tensor', dst, tp_t[:].rearrange("p (k t) -> p k t", k=8),
                 gpre_t[:, g0:g0 + 8].unsqueeze(2).to_broadcast([128, 8, 128]), ALU.mult,
                 reads=[ktp, 'gpre'], writes=wkeys)

        def load_w(dst, dkey, w_ap, c0, ncols, nk, stage=None, gcol0=None, r0=0):
            src = w_ap[r0:r0 + nk * 128, :].rearrange("(kc p) c -> p kc c", p=128)[:, :, c0:c0 + ncols]
            S.dma(dst, src, writes=[dkey + ('a',), dkey + ('b',)], eng='pool')

        def wk(dkey):
            return [dkey + ('a',), dkey + ('b',)]

        def post_norm_residual(m0, m1, mkeys, res_src, res_keys, gp_t, dst, dst_keys, xt_t, tmp_t, slot, tslot=None):
            i = nst[0] % 8
            nst[0] += 1
            c = 4 * i
            ts_ = slot if tslot is None else tslot
            S.dma(xt_t[:], res_src, reads=res_keys, writes=[('xt', slot)])
            S.op('act', 'activation', junk[:, 0:512], m0, AF.Square, accum_out=st[:, c:c + 1],
                 reads=[mkeys[0]], writes=['junk', ('st', i)])
            S.op('act', 'activation', junk[:, 512:1024], m1, AF.Square, accum_out=st[:, c + 1:c + 2],
                 reads=[mkeys[1]], writes=['junk', ('st', i)])
            S.op('dve', 'tensor_tensor', st[:, c + 2:c + 3], st[:, c:c + 1], st[:, c + 1:c + 2], ALU.add,
                 reads=[('st', i)], writes=[('st', i)])
            S.op('act', 'activation', st[:, c + 3:c + 4], st[:, c + 2:c + 3], AF.Ln, bias=EPS, scale=1.0 / DM,
                 reads=[('st', i)], writes=[('st', i)])
            S.op('act', 'activation', st[:, c:c + 1], st[:, c + 3:c + 4], AF.Exp, scale=-0.5,
                 reads=[('st', i)], writes=[('st', i)])
            S.op('dve', 'scalar_tensor_tensor', tmp_t[:, 0:512], m0, st[:, c:c + 1], gp_t[:, 0:512],
                 ALU.mult, ALU.mult, reads=[mkeys[0], ('st', i), 'gp'], writes=[('tmpa', ts_)])
            S.op('dve', 'scalar_tensor_tensor', tmp_t[:, 512:1024], m1, st[:, c:c + 1], gp_t[:, 512:1024],
                 ALU.mult, ALU.mult, reads=[mkeys[1], ('st', i), 'gp'], writes=[('tmpb', ts_)])
            S.op('pool', 'tensor_tensor', tmp_t[:], tmp_t[:], xt_t[:], ALU.add,
                 reads=[('tmpa', ts_), ('tmpb', ts_), ('xt', slot)], writes=[('tmpa', ts_), ('tmpb', ts_)])
            S.dma(dst, tmp_t[:], reads=[('tmpa', ts_), ('tmpb', ts_)], writes=dst_keys, eng='pool')

        with ExitStack() as sA:
            hT = sbt(sA, "hT", [128, 8, SEQ], BF16)
            with ExitStack() as s1:
                xt = [sbt(s1, "xt%d" % i, [128, DM], F32) for i in range(3)]
                xn = [sbt(s1, "xn%d" % i, [128, DM], BF16) for i in range(2)]
                tp = [pst(s1, "tp%d" % i, [128, DM], BF16) for i in range(2)]
                for t in range(32):
                    norm_T(x[t * 128:(t + 1) * 128, :], hT[:, :, t * 128:(t + 1) * 128],
                           xt[t % 3], xn[t % 2], tp[t % 2], t % 2, [], [('hT', t)], xkey=('xt3', t % 3))
                S.barrier()
            if stop >= 2:
                phase2(nc, S, sbt, pst, hT, w_in, biasT, scr, dbg_t, identb, load_w, wk)
        S.barrier()
        C = dict(x=x, w_in=w_in, w_out=w_out, w_gate=w_gate, w_up=w_up, w_down=w_down, convw=convw, gpost=gpost,
                 scr=scr, out=out, dbg_t=dbg_t, identb=identb, identf=identf, Umat=Umat, maskneg=maskneg,
                 onesb=onesb, onesf=onesf, hpar_t=hpar_t, cst=cst, load_w=load_w, wk=wk, norm_T=norm_T,
                 post_norm_residual=post_norm_residual, gpre_t=gpre_t, ngroups=ngroups)
        if stop >= 3:
            phase3(nc, S, sbt, pst, C)
        if stop >= 5:
            phase5(nc, S, sbt, pst, C)

        S.barrier()
        for i in range(NDS + NDP):
            if S.dval[i] > 0:
                S.wait('sp', (('d', i), S.dval[i]))
        S.emit()
        print("instr counts", S.nins)
    return nc


def phase3(nc, S, sbt, pst, C):
    x, w_in, w_out, convw, gpost, scr, out, dbg_t = C['x'], C['w_in'], C['w_out'], C['convw'], C['gpost'], C['scr'], C['out'], C['dbg_t']
    identb, identf, Umat, maskneg, onesb, onesf = C['identb'], C['identf'], C['Umat'], C['maskneg'], C['onesb'], C['onesf']
    hpar_t, cst, load_w, wk, norm_T, post_norm_residual = C['hpar_t'], C['cst'], C['load_w'], C['wk'], C['norm_T'], C['post_norm_residual']
    NG = C.get('ngroups', 8)
    with ExitStack() as s3:
        Wg = sbt(s3, "Wg", [128, 8, 2056], BF16)
        Wo = sbt(s3, "Wo", [128, 8, 1024], BF16)
        diag = sbt(s3, "diag", [128, 48, 128], BF16)
        cw = sbt(s3, "cw", [128, 48], F32)
        gpb = sbt(s3, "gpb", [128, 1024], F32)
        negA = sbt(s3, "negA", [128, 4], F32)
        masks = sbt(s3, "masks", [128, 7, 128], BF16)
        Sf = sbt(s3, "Sf", [128, 4, 128], F32)
        Sb = sbt(s3, "Sb", [128, 4, 128], BF16)
        hist = sbt(s3, "hist", [128, 12, 3], BF16)
        xt = [sbt(s3, "xt3_%d" % i, [128, DM], F32) for i in range(2)]
        xn = [sbt(s3, "xn3_%d" % i, [128, DM], BF16) for i in range(2)]
        tmp1 = sbt(s3, "tmp3_0", [128, DM], F32)
        tmp = [tmp1, tmp1]
        hTg = sbt(s3, "hTg", [128, 8, 512], BF16)
        xc = [sbt(s3, "xc%d" % i, [128, 515], BF16) for i in range(2)]
        csL = [sbt(s3, "cs%d" % i, [128, 512], F32) for i in range(4)]
        sqL = [sbt(s3, "sq%d" % i, [128, 512], BF16) for i in range(4)]
        sdL = [sbt(s3, "sd%d" % i, [128, 512], F32) for i in range(4)]
        qT = sbt(s3, "qT", [128, 4, 512], BF16)
        kT = sbt(s3, "kT", [128, 4, 512], BF16)
        vT = sbt(s3, "vT", [128, 4, 512], BF16)
        sgate = sbt(s3, "sgate", [128, 4, 512], BF16)
        obT = sbt(s3, "obT", [128, 4, 512], BF16)
        oaT = sbt(s3, "oaT", [128, 4, 512], BF16)
        sm = sbt(s3, "sm", [128, 8, 16], F32)
        bt16 = [{n: sbt(s3, "c%d_%s" % (sl_, n), [128, 4, 128], BF16) for n in
                 ('vb', 'kbg', 'ktok', 'kdec', 'attnT', 'AT', 'qdecT', 'Tm', 'TT', 'Xs', 'wT', 'vnew', 'Ao0', 'Ao1', 'Ao2')}
                for sl_ in range(2)]
        ft32 = [{n: sbt(s3, "c%d_%s" % (sl_, n), [128, 4, 128], F32) for n in
                 ('Gb', 'diagB', 'Dm', 'DTm', 'egc', 'u')} for sl_ in range(2)]
        G = [pst(s3, "G%d" % i, [128, 512], F32) for i in range(6)]
        Bb = [pst(s3, "B%d" % i, [128, 1024], BF16) for i in range(2)]
        B0 = Bb[0]

        def g4(k):
            return G[k][:].rearrange("p (h c) -> p h c", h=4)

        def bc3(ap2):
            return ap2.unsqueeze(2).to_broadcast([128, 4, 128])

        def bch(ap2):
            return ap2.unsqueeze(1).to_broadcast([128, 4, 128])

        S.dma(cw[:], convw, writes=['cw'])
        S.dma(gpb[:], gpost[0:1, :].partition_broadcast(128), writes=['gp'])
        for l in range(7):
            S.op('dve', 'tensor_copy', masks[:, l, :], cst[:, 384 + l * 128:384 + (l + 1) * 128], reads=['cst'], writes=['masks'])
        S.op('pool', 'memset', Sf[:], 0.0, writes=['Sf'])
        S.op('pool', 'memset', Sb[:], 0.0, writes=['Sb'])
        S.op('pool', 'memset', hist[:], 0.0, writes=[('hist', c) for c in range(12)])
        S.op('act', 'activation', negA[:], hpar_t[:, 0:4], AF.Exp, reads=['hpar'], writes=['negA'])
        S.op('dve', 'tensor_scalar_mul', negA[:], negA[:], -1.0, reads=['negA'], writes=['negA'])
        for ci in range(48):
            S.op('act', 'activation', diag[:, ci, :], identf, AF.Copy, scale=cw[:, ci:ci + 1],
                 reads=['cst', 'cw'], writes=[('diag', ci)])
        load_w(Wg[:, :, 2048:2056], ('Wg', 4), w_in, 2048, 8, 8)
        for pc_ in range(4):
            load_w(Wg[:, :, pc_ * 512:(pc_ + 1) * 512], ('Wg', pc_), w_in, pc_ * 512, 512, 8)
        for pc_ in range(2):
            load_w(Wo[:, :, pc_ * 512:(pc_ + 1) * 512], ('Wo', pc_), w_out, pc_ * 512, 512, 8)
        print("phase3 sbuf bytes remaining", nc.sbuf_bytes_remaining)
        WoK = [k for pc_ in range(2) for k in wk(('Wo', pc_))]

        zt, ez, gt, eb, btt, gct, egt, bgt = [sm[:, i, :].rearrange("p (a b) -> p a b", a=4) for i in range(8)]

        import os
        L3 = int(os.environ.get("P3LIM", "99"))
        for tg in range(NG):
            if L3 < 2:
                break
            tiles = [tg * 4 + i for i in range(4)]
            S.dma(obT[:], scr[:, :, tg * 512:(tg + 1) * 512], reads=[('scr', tg)], writes=['obT'])
            for i, t in enumerate(tiles):
                norm_T(x[t * 128:(t + 1) * 128, :], hTg[:, :, i * 128:(i + 1) * 128],
                       xt[i % 2], xn[i % 2], B0, i % 2, [], [('hTg', i)], tpkey='B0')
            hk = [('hTg', i) for i in range(4)]
            S.mm([(G[5][:, i * 8:(i + 1) * 8], hTg[:, kc, i * 128:(i + 1) * 128], Wg[:, kc, 2048:2056], kc == 0, kc == 7)
                  for i in range(4) for kc in range(8)], reads=hk + wk(('Wg', 4)), writes=[('G', 5)])
            ba = G[5][:, 0:32].rearrange("p (i c) -> p i c", i=4)
            S.op('dve', 'tensor_tensor', zt, ba[:, :, 4:8], hpar_t[:, 4:8].unsqueeze(1).to_broadcast([128, 4, 4]), ALU.add,
                 reads=[('G', 5), 'hpar'], writes=['zt'])
            S.op('act', 'activation', ez, zt, AF.Exp, reads=['zt'], writes=['ez'])
            S.op('act', 'activation', zt, ez, AF.Ln, bias=1.0, reads=['ez', 'zt'], writes=['zt'])
            S.op('dve', 'tensor_tensor', gt, zt, negA[:].unsqueeze(1).to_broadcast([128, 4, 4]), ALU.mult,
                 reads=['zt', 'negA'], writes=['gt'])
            S.op('act', 'activation', eb, ba[:, :, 0:4], AF.Exp, scale=-1.0, reads=[('G', 5)], writes=['eb'])
            S.op('dve', 'tensor_scalar_add', eb, eb, 1.0, reads=['eb'], writes=['eb'])
            S.op('dve', 'reciprocal', btt, eb, reads=['eb'], writes=['bt'])
            S.mm([(G[4][:, 0:16], Umat, sm[:, 2, :], True, True)], reads=['gt', 'cst'], writes=[('G', 4)])
            S.op('dve', 'tensor_copy', sm[:, 5, :], G[4][:, 0:16], reads=[('G', 4)], writes=['gct'])
            S.op('act', 'activation', egt, gct, AF.Exp, reads=['gct'], writes=['egt'])
            S.op('dve', 'tensor_tensor', bgt, btt, egt, ALU.mult, reads=['bt', 'egt'], writes=['bgt'])
            if L3 < 3:
                continue
            def stA(c):
                pj = G[c % 2]
                S.mm([(pj[:], Wg[:, kc, c * 128:(c + 1) * 128], hTg[:, kc, :], kc == 0, kc == 7) for kc in range(8)],
                     reads=hk + wk(('Wg', c // 4)), writes=[('G', c % 2)])
                if c >= 12:
                    S.op('act', 'activation', sgate[:, c - 12, :], pj[:], AF.Silu, reads=[('G', c % 2)], writes=[('sgate', c - 12)])
                    return
                xcb = xc[c % 2]
                kxc = ('xc', c % 2)
                S.op('pool', 'tensor_copy', xcb[:, 0:3], hist[:, c, :], reads=[('hist', c)], writes=[kxc])
                S.op('act', 'copy', xcb[:, 3:515], pj[:], reads=[('G', c % 2)], writes=[kxc + ('m',)])
                S.op('pool', 'tensor_copy', hist[:, c, :], xcb[:, 512:515], reads=[kxc + ('m',)], writes=[('hist', c)])

            def stB(c):
                if c >= 12:
                    return
                xcb = xc[c % 2]
                kxc = ('xc', c % 2)
                pc = G[2 + c % 2]
                S.mm([(pc[:], diag[:, c * 4 + i, :], xcb[:, i:i + 512], i == 0, i == 3) for i in range(4)],
                     reads=[kxc, kxc + ('m',)] + [('diag', c * 4 + i) for i in range(4)], writes=[('G', 2 + c % 2)])
                if c >= 8:
                    S.op('act', 'activation', vT[:, c - 8, :], pc[:], AF.Silu, reads=[('G', 2 + c % 2)], writes=[('vT', c - 8)])
                    return
                cb = c % 4
                S.op('act', 'activation', csL[cb][:], pc[:], AF.Silu, reads=[('G', 2 + c % 2)], writes=[('cs', cb)])
                S.op('dve', 'tensor_tensor', sqL[cb][:], csL[cb][:], csL[cb][:], ALU.mult, reads=[('cs', cb)], writes=[('sq', cb)])

            def stC(c):
                if c >= 8:
                    return
                cb = c % 4
                cs, sq, sd = csL[cb], sqL[cb], sdL[cb]
                S.mm([(G[4 + c % 2][:], onesb[:], sq[:], True, True)], reads=[('sq', cb), 'onesb'], writes=[('G', 4 + c % 2)])
                S.op('act', 'activation', sd[:], G[4 + c % 2][:], AF.Ln, bias=EPS, reads=[('G', 4 + c % 2)], writes=[('sd', cb)])
                S.op('act', 'activation', sd[:], sd[:], AF.Exp, scale=-0.5, reads=[('sd', cb)], writes=[('sd', cb)])
                dstT = qT if c < 4 else kT
                S.op('dve', 'scalar_tensor_tensor', dstT[:, c % 4, :], cs[:], (128.0 ** -0.5) if c < 4 else 1.0, sd[:],
                     ALU.mult, ALU.mult, reads=[('cs', cb), ('sd', cb)], writes=[('qT' if c < 4 else 'kT', c % 4)])

            for blk4 in range(4):
                cs4 = list(range(blk4 * 4, blk4 * 4 + 4))
                stA(cs4[0])
                for j_ in range(4):
                    if j_ + 1 < 4:
                        stA(cs4[j_ + 1])
                    stB(cs4[j_])
                for c_ in cs4:
                    stC(c_)
            qk = [('qT', h) for h in range(4)]
            kk = [('kT', h) for h in range(4)]
            vk = [('vT', h) for h in range(4)]
            if L3 < 4:
                continue
            def chunk_gen(ci, sl_):
                tsl = slice(ci * 128, (ci + 1) * 128)
                b_bc = bc3(btt[:, ci, :])
                bg_bc = bc3(bgt[:, ci, :])
                g_bc = bc3(gt[:, ci, :])
                gc_bc = bc3(gct[:, ci, :])
                Bs = Bb[sl_]
                kB = ('B0' if sl_ == 0 else ('tp', 77))
                P = [G[3 * sl_ + i] for i in range(3)]
                kP = [('G', 3 * sl_ + i) for i in range(3)]

                def p4(i):
                    return P[i][:].rearrange("p (h c) -> p h c", h=4)
                B0k = Bs[:, 0:512].rearrange("p (h c) -> p h c", h=4)
                B0v = Bs[:, 512:1024].rearrange("p (h c) -> p h c", h=4)
                T = bt16[sl_]
                Fq = ft32[sl_]

                def K(n):
                    return (n, sl_)
                S.tr([(Bs[:, h * 128:(h + 1) * 128], kT[:, h, tsl], identb[:]) for h in range(4)] +
                     [(Bs[:, 512 + h * 128:512 + (h + 1) * 128], vT[:, h, tsl], identb[:]) for h in range(4)],
                     reads=kk + vk + ['identb'], writes=[kB])
                S.op('dve', 'tensor_tensor', T['vb'][:], B0v, b_bc, ALU.mult, reads=[kB, 'bt'], writes=[K('vb')])
                S.op('dve', 'tensor_tensor', T['kbg'][:], B0k, bg_bc, ALU.mult, reads=[kB, 'bgt'], writes=[K('kbg')])
                S.op('act', 'copy', T['ktok'][:], B0k, reads=[kB], writes=[K('ktok')])
                yield
                S.op('act', 'copy', Fq['Gb'][:], g_bc, reads=['gt'], writes=[K('Gb')])
                for h in range(4):
                    S.op('act', 'activation', Fq['diagB'][:, h, :], identf, AF.Copy, scale=btt[:, ci, h:h + 1],
                         reads=['cst', 'bt'], writes=[K('diagB')])
                S.mm([(p4(0)[:, h, :], Fq['Gb'][:, h, :], Umat, True, True) for h in range(4)], reads=[K('Gb'), 'cst'], writes=[kP[0]])
                S.mm([(p4(1)[:, h, :], onesf[:], Fq['diagB'][:, h, :], True, True) for h in range(4)], reads=[K('diagB'), 'onesf'], writes=[kP[1]])
                yield
                S.op('dve', 'tensor_tensor', Fq['Dm'][:], p4(0), gc_bc, ALU.subtract, reads=[kP[0], 'gct'], writes=[K('Dm')])
                S.op('act', 'activation', Fq['egc'][:], p4(0), AF.Exp, reads=[kP[0]], writes=[K('egc')])
                S.op('pool', 'tensor_tensor', Fq['Dm'][:], Fq['Dm'][:], bch(maskneg), ALU.add, reads=[K('Dm'), 'cst'], writes=[K('Dm')])
                S.op('act', 'activation', Fq['DTm'][:], Fq['Dm'][:], AF.Exp, reads=[K('Dm')], writes=[K('DTm')])
                yield
                S.mm([(p4(2)[:, h, :], kT[:, h, tsl], kT[:, h, tsl], True, True) for h in range(4)], reads=kk, writes=[kP[2]])
                S.mm([(p4(0)[:, h, :], kT[:, h, tsl], qT[:, h, tsl], True, True) for h in range(4)], reads=kk + qk, writes=[kP[0]])
                yield
                S.op('dve', 'tensor_tensor', T['attnT'][:], p4(0), Fq['DTm'][:], ALU.mult, reads=[kP[0], K('DTm')], writes=[K('attnT')])
                S.op('dve', 'tensor_tensor', Fq['Gb'][:], p4(2), Fq['DTm'][:], ALU.mult, reads=[kP[2], K('DTm')], writes=[K('Gb')])
                S.op('dve', 'tensor_tensor', T['AT'][:], p4(1), Fq['Gb'][:], ALU.mult, reads=[kP[1], K('Gb')], writes=[K('AT')])
                S.op('pool', 'tensor_tensor', T['qdecT'][:], qT[:, :, tsl], Fq['egc'][:], ALU.mult, reads=qk + [K('egc')], writes=[K('qdecT')])
                for h in range(4):
                    S.op('act', 'activation', T['kdec'][:, h, :], T['ktok'][:, h, :], AF.Copy, scale=Fq['DTm'][:, h, 127:128],
                         reads=[K('ktok'), K('DTm')], writes=[K('kdec')])
                yield
                for l in range(7):
                    Ao = T['Ao%d' % (l % 3)]
                    ak = K('Ao%d' % (l % 3))
                    S.op('pool', 'tensor_tensor', Ao[:], T['AT'][:], bch(masks[:, l, :]), ALU.mult, reads=[K('AT'), 'masks'], writes=[ak])
                    if l == 0:
                        S.op('dve', 'tensor_tensor', T['TT'][:], bch(identb[:]), Ao[:], ALU.subtract, reads=['identb', ak], writes=[K('TT')])
                        S.tr([(Bs[:, h * 128:(h + 1) * 128], Ao[:, h, :], identb[:]) for h in range(4)], reads=[ak, 'identb'], writes=[kB])
                        S.op('dve', 'tensor_tensor', T['Tm'][:], bch(identb[:]), B0k, ALU.subtract, reads=['identb', kB], writes=[K('Tm')])
                        yield
                        continue
                    S.mm([(p4(0)[:, h, :], Ao[:, h, :], T['Tm'][:, h, :], True, True) for h in range(4)], reads=[ak, K('Tm')], writes=[kP[0]])
                    S.op('act', 'copy', T['Xs'][:], p4(0), reads=[kP[0]], writes=[K('Xs')])
                    yield
                    if l < 6:
                        S.mm([(p4(1)[:, h, :], T['TT'][:, h, :], T['Xs'][:, h, :], True, True) for h in range(4)], reads=[K('TT'), K('Xs')], writes=[kP[1]])
                    S.mm([(p4(2)[:, h, :], T['Xs'][:, h, :], T['TT'][:, h, :], True, True) for h in range(4)], reads=[K('TT'), K('Xs')], writes=[kP[2]])
                    if l < 6:
                        S.op('dve', 'tensor_tensor', T['Tm'][:], T['Tm'][:], p4(1), ALU.subtract, reads=[K('Tm'), kP[1]], writes=[K('Tm')])
                    S.op('dve', 'tensor_tensor', T['TT'][:], T['TT'][:], p4(2), ALU.subtract, reads=[K('TT'), kP[2]], writes=[K('TT')])
                    yield
                S.mm([(p4(0)[:, h, :], T['TT'][:, h, :], T['vb'][:, h, :], True, True) for h in range(4)], reads=[K('TT'), K('vb')], writes=[kP[0]])
                S.mm([(p4(1)[:, h, :], T['kbg'][:, h, :], T['TT'][:, h, :], True, True) for h in range(4)], reads=[K('TT'), K('kbg')], writes=[kP[1]])
                S.op('act', 'copy', Fq['u'][:], p4(0), reads=[kP[0]], writes=[K('u')])
                S.op('dve', 'tensor_copy', T['wT'][:], p4(1), reads=[kP[1]], writes=[K('wT')])
                yield
                S.mm([(p4(2)[:, h, :], T['wT'][:, h, :], Sb[:, h, :], True, True) for h in range(4)], reads=[K('wT'), 'Sb'], writes=[kP[2]])
                S.op('dve', 'tensor_tensor', T['vnew'][:], Fq['u'][:], p4(2), ALU.subtract, reads=[K('u'), kP[2]], writes=[K('vnew')])
                it = []
                for h in range(4):
                    it += [(p4(0)[:, h, :], Sb[:, h, :], T['qdecT'][:, h, :], True, False),
                           (p4(0)[:, h, :], T['vnew'][:, h, :], T['attnT'][:, h, :], False, True)]
                S.mm(it, reads=['Sb', K('qdecT'), K('vnew'), K('attnT')], writes=[kP[0]])
                S.mm([(p4(1)[:, h, :], T['kdec'][:, h, :], T['vnew'][:, h, :], True, True) for h in range(4)], reads=[K('kdec'), K('vnew')], writes=[kP[1]])
                S.op('pool', 'tensor_tensor', Fq['Dm'][:], Sf[:], Fq['egc'][:, :, 127:128].to_broadcast([128, 4, 128]), ALU.mult,
                     reads=['Sf', K('egc')], writes=[K('Dm')])
                S.op('dve', 'tensor_tensor', Sf[:], Fq['Dm'][:], p4(1), ALU.add, reads=[K('Dm'), kP[1]], writes=['Sf'])
                S.op('act', 'copy', Sb[:], Sf[:], reads=['Sf'], writes=['Sb'])
                S.op('act', 'activation', T['Xs'][:], p4(0), AF.Square, reads=[kP[0]], writes=[K('Xs')])
                S.mm([(P[2][:], onesb[:], T['Xs'][:].rearrange("p h c -> p (h c)"), True, True)], reads=[K('Xs'), 'onesb'], writes=[kP[2]])
                S.op('act', 'activation', Fq['DTm'][:], p4(2), AF.Ln, bias=EPS, scale=1.0 / 128, reads=[kP[2]], writes=[K('DTm')])
                S.op('act', 'activation', Fq['DTm'][:], Fq['DTm'][:], AF.Exp, scale=-0.5, reads=[K('DTm')], writes=[K('DTm')])
                S.op('dve', 'scalar_tensor_tensor', Fq['u'][:], p4(0), hpar_t[:, 8:9], Fq['DTm'][:], ALU.mult, ALU.mult,
                     reads=[kP[0], K('DTm'), 'hpar'], writes=[K('u')])
                S.op('dve', 'tensor_tensor', oaT[:, :, tsl], Fq['u'][:], sgate[:, :, tsl], ALU.mult,
                     reads=[K('u')] + [('sgate', h) for h in range(4)], writes=[('oaT', ci)])
                if dbg_t:
                    S.dma(dbg_t['oa'][:, :, tg * 512 + ci * 128:tg * 512 + (ci + 1) * 128], oaT[:, :, tsl],
                          reads=[('oaT', ci)], writes=[('dbgoa', ci, tg)])
                yield
                t = tg * 4 + ci
                for half in range(2):
                    it = []
                    for fc in range(8):
                        src = oaT[:, fc, tsl] if fc < 4 else obT[:, fc - 4, tsl]
                        it.append((P[half][:], src, Wo[:, fc, half * 512:(half + 1) * 512], fc == 0, fc == 7))
                    S.mm(it, reads=[('oaT', ci), 'obT'] + WoK, writes=[kP[half]])
                post_norm_residual(P[0][:], P[1][:], [kP[0], kP[1]], x[t * 128:(t + 1) * 128, :], [],
                                   gpb, out[t * 128:(t + 1) * 128, :], [('out', t)], xt[sl_], tmp[0], sl_, tslot=0)
                yield

            for pair in ((0, 1), (2, 3)):
                gens = [chunk_gen(pair[0], 0), chunk_gen(pair[1], 1)]
                live = list(gens)
                while live:
                    for g_ in list(live):
                        try:
                            next(g_)
                        except StopIteration:
                            live.remove(g_)
        S.barrier()


def phase5(nc, S, sbt, pst, C):
    w_gate, w_up, w_down, gpost, out = C['w_gate'], C['w_up'], C['w_down'], C['gpost'], C['out']
    load_w, wk, norm_T, post_norm_residual = C['load_w'], C['wk'], C['norm_T'], C['post_norm_residual']
    NG = C.get('ngroups', 8)
    with ExitStack() as s5:
        Wga = sbt(s5, "Wga", [128, 8, DFF], BF16)
        Wup = sbt(s5, "Wup", [128, 8, DFF], BF16)
        Wdn = sbt(s5, "Wdn", [128, 22, DM], BF16)
        gpb = sbt(s5, "gpb5", [128, DM], F32)
        xt = [sbt(s5, "xt5_%d" % i, [128, DM], F32) for i in range(2)]
        xn = [sbt(s5, "xn5_%d" % i, [128, DM], BF16) for i in range(2)]
        tmp = [sbt(s5, "tmp5_%d" % i, [128, DM], F32) for i in range(2)]
        h2TL = [sbt(s5, "h2T%d" % i, [128, 8, 512], BF16) for i in range(2)]
        actT = sbt(s5, "actT", [128, 22, 512], BF16)
        sg = [sbt(s5, "sg%d" % i, [128, 512], BF16) for i in range(2)]
        F = [pst(s5, "F%d" % i, [128, 512], F32) for i in range(7)]
        TP = pst(s5, "TP5", [128, 1024], BF16)
        print("phase5 sbuf bytes remaining", nc.sbuf_bytes_remaining)
        S.dma(gpb[:], gpost[1:2, :].partition_broadcast(128), writes=['gp'])
        for pc_ in range(4):
            load_w(Wga[:, :, pc_ * 704:(pc_ + 1) * 704], ('Wga', pc_), w_gate, pc_ * 704, 704, 8)
            load_w(Wup[:, :, pc_ * 704:(pc_ + 1) * 704], ('Wup', pc_), w_up, pc_ * 704, 704, 8)
        for pc_ in range(11):
            load_w(Wdn[:, pc_ * 2:(pc_ + 1) * 2, :], ('Wdn', pc_), w_down, 0, DM, 2, r0=pc_ * 256)
        WdnK = [k for pc_ in range(11) for k in wk(('Wdn', pc_))]
        def prep1(tg, i):
            t = tg * 4 + i
            norm_T(out[t * 128:(t + 1) * 128, :], h2TL[tg % 2][:, :, i * 128:(i + 1) * 128],
                   xt[i % 2], xn[i % 2], TP, i % 2, [('out', t)], [('h2T', tg % 2, i)], tpkey=('tp', 9), g0=8)

        def prep(tg):
            for i in range(4):
                prep1(tg, i)
        prep(0)
        for tg in range(NG):
            tiles = [tg * 4 + i for i in range(4)]
            h2T = h2TL[tg % 2]
            hk = [('h2T', tg % 2, i) for i in range(4)]
            for fc in range(22):
                pg = F[(2 * fc) % 4]
                pu = F[(2 * fc + 1) % 4]
                kg = ('F', (2 * fc) % 4)
                ku = ('F', (2 * fc + 1) % 4)
                S.mm([(pg[:], Wga[:, kc, fc * 128:(fc + 1) * 128], h2T[:, kc, :], kc == 0, kc == 7) for kc in range(8)],
                     reads=hk + wk(('Wga', (fc * 128) // 704)) + wk(('Wga', (fc * 128 + 127) // 704)), writes=[kg])
                S.mm([(pu[:], Wup[:, kc, fc * 128:(fc + 1) * 128], h2T[:, kc, :], kc == 0, kc == 7) for kc in range(8)],
                     reads=hk + wk(('Wup', (fc * 128) // 704)) + wk(('Wup', (fc * 128 + 127) // 704)), writes=[ku])
                S.op('act', 'activation', sg[fc % 2][:], pg[:], AF.Silu, reads=[kg], writes=[('sg', fc % 2)])
                S.op('dve', 'tensor_tensor', actT[:, fc, :], pu[:], sg[fc % 2][:], ALU.mult,
                     reads=[ku, ('sg', fc % 2)], writes=[('actT', fc)])
            ak = [('actT', fc) for fc in range(22)]
            for i, t in enumerate(tiles):
                mb = (4, 5) if i % 2 == 0 else (6, 0)
                for half in range(2):
                    S.mm([(F[mb[half]][:], actT[:, fc, i * 128:(i + 1) * 128], Wdn[:, fc, half * 512:(half + 1) * 512],
                           fc == 0, fc == 21) for fc in range(22)], reads=ak + WdnK, writes=[('F', mb[half])])
                if tg + 1 < NG:
                    prep1(tg + 1, i)
                post_norm_residual(F[mb[0]][:], F[mb[1]][:], [('F', mb[0]), ('F', mb[1])], out[t * 128:(t + 1) * 128, :],
                                   [('out', t)], gpb, out[t * 128:(t + 1) * 128, :], [('out', t)], xt[i % 2], tmp[i % 2], i % 2)
        S.barrier()


def phase2(nc, S, sbt, pst, hT, w_in, biasT, scr, dbg_t, identb, load_w, wk):
    with ExitStack() as s2:
        QT2 = sbt(s2, "QT2", [128, 2, SEQ], BF16)
        QTA = QT2[:, 0, :]
        QTB = QT2[:, 1, :]
        tmpz = sbt(s2, "tmpz", [128, 512], F32)
        nzs = [sbt(s2, "nzs%d" % i, [128, 2, 128], F32) for i in range(2)]
        KT = sbt(s2, "KT", [128, SEQ], BF16)
        VT = sbt(s2, "VT", [128, SEQ], BF16)
        V3 = [sbt(s2, "V3_%d" % i, [128, 32, 256], BF16) for i in range(2)]
        acc = sbt(s2, "acc", [128, 2, SEQ], F32)
        Et = sbt(s2, "Et", [128, 3, 512], BF16)
        ebias = sbt(s2, "ebias", [128, 512], F32)
        pt = [sbt(s2, "pt%d" % i, [128, 512], BF16) for i in range(5)]
        wqkvL = [sbt(s2, "wqkv%d" % i, [128, 8, 384], BF16) for i in range(2)]
        onesAB = sbt(s2, "onesAB", [128, 256], BF16)
        obuf = [sbt(s2, "obuf%d" % i, [128, 512], BF16) for i in range(2)]
        sp = [pst(s2, "sp%d" % i, [128, 512], F32) for i in range(5)]
        pj = [sp[3], sp[4]]
        nz = [pst(s2, "nz%d" % i, [128, 256], F32)[:] for i in range(2)]
        vtp_all = pst(s2, "vtp", [128, 512], BF16)
        vtp = [vtp_all[:], vtp_all[:]]
        S.op('pool', 'memset', onesAB[:], 0.0, writes=['onesAB'])
        S.op('pool', 'memset', onesAB[:, 0:64], 1.0, reads=['onesAB'], writes=['onesAB'])
        S.op('pool', 'memset', onesAB[:, 192:256], 1.0, reads=['onesAB'], writes=['onesAB'])
        for i in range(2):
            S.op('pool', 'memset', V3[i][:, :, 64:192], 1.0, writes=[('V3z', i)])
        print("phase2 sbuf bytes remaining", nc.sbuf_bytes_remaining)
        S.op('pool', 'memset', QTA[64:128, :], 0.0, writes=['QTz'])
        S.op('pool', 'memset', QTB[0:64, :], 0.0, writes=['QTzb'])
        pcount = 0
        ucount = 0
        zcount = 0
        import os
        LIM = int(os.environ.get("P2LIM", "99"))
        def ldw(hq):
            for j, c0 in enumerate((2056 + hq * 128, 2568 + hq * 128, 3080 + hq * 128)):
                load_w(wqkvL[hq % 2][:, :, j * 128:(j + 1) * 128], ('wqkv', hq % 2, j), w_in, c0, 128, 8)

        def proj(hq):
            wqkv = wqkvL[hq % 2]
            for tg in range(8):
                for j in range(3):
                    b = (tg * 3 + j) % 2
                    S.mm([(pj[b][:], wqkv[:, kc, j * 128:(j + 1) * 128], hT[:, kc, tg * 512:(tg + 1) * 512],
                           kc == 0, kc == 7) for kc in range(8)],
                         reads=wk(('wqkv', hq % 2, j)) + [('hT', tg * 4 + i) for i in range(4)], writes=[('sp', 3 + b)])
                    sl = slice(tg * 512, (tg + 1) * 512)
                    if j == 0:
                        S.op('act', 'mul', QTA[0:64, sl], pj[b][0:64, :], 0.125, reads=[('sp', 3 + b), 'QTz'], writes=[('QT', tg)])
                        S.op('act', 'mul', QTB[64:128, sl], pj[b][64:128, :], 0.125, reads=[('sp', 3 + b), 'QTzb'], writes=[('QTb', tg)])
                    elif j == 1:
                        S.op('act', 'copy', KT[:, sl], pj[b][:], reads=[('sp', 3 + b)], writes=[('KT', tg)])
                    else:
                        S.op('act', 'copy', VT[:, sl], pj[b][:], reads=[('sp', 3 + b)], writes=[('VT', tg)])

        NHP = 4 if LIM >= 9 else 1
        ldw(0)
        proj(0)
        for hp in range(NHP):
            if hp + 1 < NHP:
                ldw(hp + 1)
            if LIM < 2:
                continue
            for p in range(3):
                S.dma(ebias[:], biasT[hp, :, p * 512:(p + 1) * 512], writes=['ebias'])
                S.op('act', 'activation', Et[:, p, :], ebias[:], AF.Exp, reads=['ebias'], writes=[('E', p)])
            if LIM < 3:
                continue
            PATS = (1, 4, 16)

            def mk(d):
                nb_ = 32 // d

                def tok_(blk):
                    r, n = divmod(blk, nb_)
                    s_ = n * 128 * d + r
                    return slice(s_, s_ + 127 * d + 1, d)

                def tgs_(blk):
                    r, n = divmod(blk, nb_)
                    s_ = n * 128 * d + r
                    return list(range(s_ // 512, (s_ + 127 * d) // 512 + 1))
                return nb_, tok_, tgs_

            def emit_V(p_, vslot_, g_lo, g_hi):
                nb_, tok_, tgs_ = mk(PATS[p_])
                vb_ = V3[vslot_]
                for g4 in range(g_lo, g_hi):
                    rk = set()
                    for i in range(4):
                        rk.update(tgs_(g4 * 4 + i))
                    S.tr([(vtp[0][:, i * 128:(i + 1) * 128], VT[:, tok_(g4 * 4 + i)], identb[:]) for i in range(4)],
                         reads=[('VT', t) for t in sorted(rk)] + ['identb'], writes=[('vtp', 0)])
                    srcv = vtp[0].rearrange("p (b c) -> p b c", b=4)
                    S.op('dve', 'tensor_copy', vb_[:, g4 * 4:(g4 + 1) * 4, 0:64], srcv[:, :, 0:64],
                         reads=[('vtp', 0), ('V3z', vslot_)], writes=[('V3', vslot_, g4)])
                    S.op('dve', 'tensor_copy', vb_[:, g4 * 4:(g4 + 1) * 4, 192:256], srcv[:, :, 64:128],
                         reads=[('vtp', 0), ('V3z', vslot_)], writes=[('V3b', vslot_, g4)])

            emit_V(0, pcount % 2, 0, 8)
            for p, d in enumerate(PATS):
                nb, tok, tgs = mk(d)
                vslot = pcount % 2
                pcount += 1
                vb = V3[vslot]

                if LIM < 4:
                    continue

                def emit_S(blk, u):
                    r, n = divmod(blk, nb)
                    s = u % 5
                    qs = tok(blk)
                    items = [(sp[s][:, 0:256], KT[:, qs], QT2[:, :, qs], True, True)]
                    rk = set(tgs(blk))
                    if n > 0:
                        ks = tok(blk - 1)
                        rk.update(tgs(blk - 1))
                        items += [(sp[s][:, 256:512], KT[:, ks], QT2[:, :, qs], True, True)]
                    VAR = os.environ.get("P2VAR", "")
                    if VAR == "a":
                        items = [it for k, it in enumerate(items) if k % 2 == 0]
                    if VAR == "b":
                        items = [it for k, it in enumerate(items) if k % 2 == 1]
                    if VAR == "c":
                        for it in items:
                            S.mm([it], reads=[('QT', t) for t in sorted(rk)] + [('QTb', t) for t in sorted(rk)] + [('KT', t) for t in sorted(rk)], writes=[('sp', s)])
                        items = []
                    if items:
                      S.mm(items, reads=[('QT', t) for t in sorted(rk)] + [('QTb', t) for t in sorted(rk)] + [('KT', t) for t in sorted(rk)],
                         writes=[('sp', s)])
                    wd = 512 if n > 0 else 256
                    S.op('act', 'activation', pt[s][:, 0:wd], sp[s][:, 0:wd], AF.Exp,
                         reads=[('sp', s)], writes=[('pt', s)])
                    S.op('dve', 'tensor_tensor', pt[s][:, 0:wd], pt[s][:, 0:wd], Et[:, p, 0:wd], ALU.mult,
                         reads=[('pt', s), ('E', p)], writes=[('pt', s)])

                def emit_PV(blk, u, zc):
                    SUB = int(os.environ.get("P2SUB", "9"))
                    if SUB < 2:
                        return
                    r, n = divmod(blk, nb)
                    s = u % 5
                    zs = zc % 2
                    P_ = pt[s]
                    itN = [(nz[zs][:, 0:128], vb[:, blk, 0:128], P_[:, 0:128], True, n == 0)]
                    itZ = [(nz[zs][:, 128:256], vb[:, blk, 128:256], P_[:, 128:256], True, n == 0)]
                    rk = [('pt', s), ('V3', vslot, blk // 4), ('V3b', vslot, blk // 4), ('V3z', vslot)]
                    if n > 0:
                        itN += [(nz[zs][:, 0:128], vb[:, blk - 1, 0:128], P_[:, 256:384], False, True)]
                        itZ += [(nz[zs][:, 128:256], vb[:, blk - 1, 128:256], P_[:, 384:512], False, True)]
                        rk += [('V3', vslot, (blk - 1) // 4), ('V3b', vslot, (blk - 1) // 4)]
                    S.mm(itN + itZ, reads=rk, writes=[('nz', zs)])
                    if SUB < 3:
                        return
                    qs = tok(blk)
                    accv = acc[:, :, qs]
                    nzv = nz[zs].rearrange("p (a q) -> p a q", a=2)
                    akeys = [('acc', t) for t in range(qs.start // 128, (qs.start + 127 * d) // 128 + 1)]
                    odd = (blk % 2 == 1)
                    if p == 0:
                        if odd:
                            S.op('act', 'copy', accv, nzv, reads=[('nz', zs)], writes=akeys)
                        else:
                            S.op('dve', 'tensor_copy', accv, nzv, reads=[('nz', zs)], writes=akeys)
                    elif odd:
                        k_ = (blk // 2) % 2
                        S.op('act', 'copy', nzs[k_][:], nzv, reads=[('nz', zs)], writes=[('nzs', k_)])
                        S.op('pool', 'tensor_tensor', accv, nzs[k_][:], accv, ALU.add,
                             reads=[('nzs', k_)] + akeys, writes=akeys)
                    else:
                        S.op('dve', 'tensor_tensor', accv, nzv, accv, ALU.add,
                             reads=[('nz', zs)] + akeys, writes=akeys)

                LA = 4
                for b_ in range(LA):
                    emit_S(b_, ucount + b_)
                for blk in range(32):
                    if p + 1 < 3 and blk in (8, 12, 16, 20):
                        g0_ = (blk - 8) // 2
                        emit_V(p + 1, pcount % 2, g0_, g0_ + 2)
                    if blk + LA < 32:
                        emit_S(blk + LA, ucount + blk + LA)
                    emit_PV(blk, ucount + blk, zcount)
                    zcount += 1
                ucount += 32
            if LIM < 6:
                continue
            if hp + 1 < NHP:
                proj(hp + 1)
            for tg in range(8):
                sl = slice(tg * 512, (tg + 1) * 512)
                ak = [('acc', tg * 4 + i) for i in range(4)]
                ob = obuf[tg % 2]
                S.op('act', 'activation', tmpz[0:64, :], acc[64:128, 0, sl], AF.Ln, reads=ak, writes=['tmpza'])
                S.op('act', 'activation', tmpz[64:128, :], acc[0:64, 1, sl], AF.Ln, reads=ak, writes=['tmpzb'])
                S.op('act', 'activation', tmpz[:], tmpz[:], AF.Exp, scale=-1.0, reads=['tmpza', 'tmpzb'], writes=['tmpza', 'tmpzb'])
                S.op('dve', 'tensor_tensor', ob[0:64, :], acc[0:64, 0, sl], tmpz[0:64, :], ALU.mult,
                     reads=ak + ['tmpza'], writes=[('obuf', tg % 2)])
                S.op('pool', 'tensor_tensor', ob[64:128, :], acc[64:128, 1, sl], tmpz[64:128, :], ALU.mult,
                     reads=ak + ['tmpzb'], writes=[('obuf', tg % 2, 'b')])
                S.dma(scr[:, hp, sl], ob[:], reads=[('obuf', tg % 2), ('obuf', tg % 2, 'b')], writes=[('scr', tg)], eng='pool')
                if dbg_t:
                    S.dma(dbg_t['ob'][:, hp, sl], ob[:], reads=[('obuf', tg % 2), ('obuf', tg % 2, 'b')], writes=[('dbgob', hp, tg)])
        S.barrier()


def _t5_bucket(dist):
    d = np.maximum(dist, 1).astype(np.float32)
    log_b = 16 + (np.log(d / np.float32(16)) / np.float32(math.log(2048 / 16)) * np.float32(16)).astype(np.int32)
    return np.where(dist < 16, dist, np.minimum(log_b, 31))


def _host_layouts(inp):
    f32 = np.float32
    gpre = np.zeros((128, 16), f32)
    gpre[:, 0:8] = inp['g_mix_pre'][0].reshape(8, 128).T
    gpre[:, 8:16] = inp['g_ffn_pre'][0].reshape(8, 128).T
    gpost = np.stack([inp['g_mix_post'][0], inp['g_ffn_post'][0]]).astype(f32)
    convw = np.ascontiguousarray(inp['conv_w'][0].reshape(4, 12, 128).transpose(2, 1, 0)).reshape(128, 48).astype(f32)
    hpar = np.zeros((128, 16), f32)
    hpar[:, 0:4] = inp['a_log'][0][None, :]
    hpar[:, 4:8] = inp['dt_bias'][0][None, :]
    hpar[:, 8] = inp['onorm_g'][0]
    rb = inp['rel_bias'].astype(f32)
    jj = np.arange(128)[:, None]
    ii = np.arange(128)[None, :]
    biasT = np.full((4, 128, 3, 4, 128), NEG, f32)
    for p, dil in enumerate((1, 4, 16)):
        bs = rb[_t5_bucket(np.arange(129) * dil)]
        rel_c = ii - jj
        rel_p = ii + 128 - jj
        for h in range(8):
            cur = np.where(rel_c >= 0, bs[np.clip(rel_c, 0, 128), h], f32(NEG))
            prv = np.where(rel_p <= 128, bs[np.clip(rel_p, 0, 128), h], f32(NEG))
            biasT[h // 2, :, p, (h % 2), :] = cur
            biasT[h // 2, :, p, 2 + (h % 2), :] = prv
    biasT = biasT.reshape(4, 128, 1536)
    consts = np.zeros((128, 1280), f32)
    consts[:, 0:128] = np.eye(128, dtype=f32)
    consts[:, 128:256] = (jj <= ii).astype(f32)
    consts[:, 256:384] = np.where(ii >= jj, 0.0, NEG)
    for l in range(7):
        b = 1 << l
        m = ((ii // (2 * b)) == (jj // (2 * b))) & ((ii % (2 * b)) >= b) & ((jj % (2 * b)) < b)
        consts[:, 384 + l * 128:384 + (l + 1) * 128] = m.astype(f32)
    return dict(gpre=gpre, gpost=gpost, convw=convw, hpar=hpar, biasT=biasT, consts=consts)


def kernel(**inputs):
    inp = {k: np.asarray(v) for k, v in inputs.items()}
    lay = _host_layouts(inp)
    nc = build_nc()
    shared = dict(w_in=np.ascontiguousarray(inp['w_in'][0]), w_out=np.ascontiguousarray(inp['w_out'][0]),
                  w_gate=np.ascontiguousarray(inp['w_gate'][0]), w_up=np.ascontiguousarray(inp['w_up'][0]),
                  w_down=np.ascontiguousarray(inp['w_down'][0]), **lay)
    in_maps = [dict(shared, x=np.ascontiguousarray(inp['x'][b])) for b in range(8)]
    res = run_bass_kernel_spmd(nc, in_maps, core_ids=list(range(8)))
    return np.stack([np.asarray(r["out"]) for r in res.results], axis=0).astype(np.float32)
```
